# Optimizing a Trainium2 kernel written in Bass

```python
import math
import jax, jax.numpy as jnp
from jax import lax
import numpy as np

D_MODEL = 1024
BATCH = 4
SEQ = 4096
DEPTH = 1

SSM_EXPAND = 2
SSM_D_INNER = SSM_EXPAND * D_MODEL
SSM_HEAD_DIM = 64
SSM_HEADS = SSM_D_INNER // SSM_HEAD_DIM
SSM_GROUPS = 4
SSM_HEADS_PER_GROUP = SSM_HEADS // SSM_GROUPS
SSM_D_STATE = 128
SSM_CONV = 4
SSM_CHUNK = 128
SSM_BC_DIM = SSM_GROUPS * SSM_D_STATE
SSM_CONV_DIM = SSM_D_INNER + 2 * SSM_BC_DIM
SSM_NORM_GROUP = SSM_D_INNER // SSM_GROUPS
SSM_DT_MIN = 0.001
SSM_DT_MAX = 0.1

LRU_WIDTH = 1280
LRU_BLOCKS = 10
LRU_BLOCK = LRU_WIDTH // LRU_BLOCKS
LRU_CONV = 4
LRU_C = 8.0

FFN_HIDDEN = -(-8 * D_MODEL // (3 * 256)) * 256

RMS_EPS = 1e-6

N_GATES = 2 * D_MODEL
IN_PROJ_DIM = N_GATES + SSM_D_INNER + SSM_CONV_DIM + SSM_HEADS + 2 * LRU_WIDTH
IN_SPLITS = (
    N_GATES,
    N_GATES + SSM_D_INNER,
    N_GATES + SSM_D_INNER + SSM_CONV_DIM,
    N_GATES + SSM_D_INNER + SSM_CONV_DIM + SSM_HEADS,
    N_GATES + SSM_D_INNER + SSM_CONV_DIM + SSM_HEADS + LRU_WIDTH,
)

kernel_name = "hybrid_ssd_rglru_gated_block"


def rmsnorm(x, w, eps=RMS_EPS):
    xf = x.astype(jnp.float32)
    y = xf * lax.rsqrt(jnp.mean(xf * xf, axis=-1, keepdims=True) + eps)
    return (y * w.astype(jnp.float32)).astype(x.dtype)


def causal_dwconv(u, w, b):
    k_width = w.shape[0]
    length = u.shape[1]
    up = jnp.pad(u, ((0, 0), (k_width - 1, 0), (0, 0)))
    out = b + up[:, 0:length] * w[0]
    for k in range(1, k_width):
        out = out + up[:, k:k + length] * w[k]
    return out


def segsum_exp(cs):
    t = cs.shape[-1]
    mask = jnp.tril(jnp.ones((t, t), dtype=bool))
    diff = cs[..., :, None] - cs[..., None, :]
    return jnp.exp(jnp.where(mask, diff, -jnp.inf))


def ssd_chunked(xdt, dA, bm, cm):
    b, length, g, r, p = xdt.shape
    n = bm.shape[-1]
    l = SSM_CHUNK
    c = length // l
    xdt = xdt.reshape(b, c, l, g, r, p)
    dA = dA.reshape(b, c, l, g, r)
    bm = bm.reshape(b, c, l, g, n)
    cm = cm.reshape(b, c, l, g, n)

    cs = jnp.cumsum(dA, axis=2)
    lmat = segsum_exp(jnp.moveaxis(cs, 2, -1))
    cb = jnp.einsum('bclgn,bcsgn->bcgls', cm, bm)
    y_diag = jnp.einsum('bcgls,bcgrls,bcsgrp->bclgrp', cb, lmat, xdt)
    decay_states = jnp.exp(cs[:, :, -1:] - cs)
    states = jnp.einsum('bclgn,bclgr,bclgrp->bcgrpn', bm, decay_states, xdt)
    chunk_tot = jnp.moveaxis(cs[:, :, -1], 1, -1)
    chunk_cs = jnp.cumsum(jnp.pad(chunk_tot, ((0, 0), (0, 0), (0, 0), (1, 0))), axis=-1)
    decay_chunk = segsum_exp(chunk_cs)
    states = jnp.concatenate([jnp.zeros_like(states[:, :1]), states], axis=1)
    states_in = jnp.einsum('bgrzc,bcgrpn->bzgrpn', decay_chunk[..., :-1, :], states)
    y_off = jnp.einsum('bclgn,bcgrpn,bclgr->bclgrp', cm, states_in, jnp.exp(cs))
    return (y_diag + y_off).reshape(b, length, g, r, p)


def mamba2_mixer(z, xbc, dt_raw, conv_w, conv_b, dt_bias, a_log, d_skip, norm_w):
    b, length, _ = z.shape
    f32 = jnp.float32
    xbc = jax.nn.silu(causal_dwconv(xbc, conv_w, conv_b))
    xs, bm, cm = jnp.split(xbc, [SSM_D_INNER, SSM_D_INNER + SSM_BC_DIM], axis=-1)
    dt = jax.nn.softplus(dt_raw.astype(f32) + dt_bias.astype(f32))
    a = -jnp.exp(a_log.astype(f32))
    g, r, p, n = SSM_GROUPS, SSM_HEADS_PER_GROUP, SSM_HEAD_DIM, SSM_D_STATE
    xs_h = xs.astype(f32).reshape(b, length, g, r, p)
    dt_h = dt.reshape(b, length, g, r)
    y = ssd_chunked(xs_h * dt_h[..., None], dt_h * a.reshape(g, r),
                    bm.astype(f32).reshape(b, length, g, n),
                    cm.astype(f32).reshape(b, length, g, n))
    y = y + xs_h * d_skip.astype(f32).reshape(g, r, 1)
    y = y.reshape(b, length, SSM_D_INNER) * jax.nn.silu(z.astype(f32))
    yg = y.reshape(b, length, SSM_GROUPS, SSM_NORM_GROUP)
    yg = yg * lax.rsqrt(jnp.mean(yg * yg, axis=-1, keepdims=True) + RMS_EPS)
    y = yg.reshape(b, length, SSM_D_INNER) * norm_w.astype(f32)
    return y.astype(z.dtype)


def rglru_mixer(xl, yl, conv_w, conv_b, w_r, b_r, w_i, b_i, lam):
    b, length, _ = xl.shape
    f32 = jnp.float32
    u = causal_dwconv(xl, conv_w, conv_b)
    ub = u.reshape(b, length, LRU_BLOCKS, LRU_BLOCK)
    r_gate = jax.nn.sigmoid((jnp.einsum('blhi,hij->blhj', ub, w_r).reshape(b, length, LRU_WIDTH) + b_r).astype(f32))
    i_gate = jax.nn.sigmoid((jnp.einsum('blhi,hij->blhj', ub, w_i).reshape(b, length, LRU_WIDTH) + b_i).astype(f32))
    log_a = -LRU_C * r_gate * jax.nn.softplus(-lam.astype(f32))
    a_t = jnp.exp(log_a)
    b_t = jnp.sqrt(-jnp.expm1(2.0 * log_a)) * (i_gate * u.astype(f32))

    def combine(left, right):
        a1, b1 = left
        a2, b2 = right
        return a1 * a2, a2 * b1 + b2

    _, h = lax.associative_scan(combine, (a_t, b_t), axis=1)
    out = h * jax.nn.gelu(yl.astype(f32), approximate=True)
    return out.astype(xl.dtype)


def setup_inputs(seed: int = 0) -> dict:
    key = jax.random.key(seed)
    ks = jax.random.split(key, 32)
    f32 = jnp.float32
    nrm = lambda k, shape, scale: jax.random.normal(k, shape, f32) * scale
    L = DEPTH

    x = jax.random.normal(ks[0], (BATCH, SEQ, D_MODEL), f32)
    norm1_w = 1.0 + nrm(ks[1], (L, D_MODEL), 0.05)
    w_in = nrm(ks[2], (L, D_MODEL, IN_PROJ_DIM), D_MODEL ** -0.5)
    b_branch_gate = nrm(ks[3], (L, N_GATES), 0.1)

    ssm_conv_w = nrm(ks[4], (L, SSM_CONV, SSM_CONV_DIM), SSM_CONV ** -0.5)
    ssm_conv_b = nrm(ks[5], (L, SSM_CONV_DIM), 0.02)
    u = jax.random.uniform(ks[6], (L, SSM_HEADS), f32)
    dt0 = jnp.exp(u * (math.log(SSM_DT_MAX) - math.log(SSM_DT_MIN)) + math.log(SSM_DT_MIN))
    dt0 = jnp.maximum(dt0, 1e-4)
    ssm_dt_bias = dt0 + jnp.log(-jnp.expm1(-dt0))
    ssm_a_log = jnp.log(jax.random.uniform(ks[7], (L, SSM_HEADS), f32, 1.0, 16.0))
    ssm_d = 1.0 + nrm(ks[8], (L, SSM_HEADS), 0.1)
    ssm_norm_w = 1.0 + nrm(ks[9], (L, SSM_D_INNER), 0.05)
    w_out_ssm = nrm(ks[10], (L, SSM_D_INNER, D_MODEL), SSM_D_INNER ** -0.5)

    lru_conv_w = nrm(ks[11], (L, LRU_CONV, LRU_WIDTH), LRU_CONV ** -0.5)
    lru_conv_b = nrm(ks[12], (L, LRU_WIDTH), 0.02)
    lru_w_r = nrm(ks[13], (L, LRU_BLOCKS, LRU_BLOCK, LRU_BLOCK), LRU_BLOCK ** -0.5)
    lru_b_r = nrm(ks[14], (L, LRU_WIDTH), 0.02)
    lru_w_i = nrm(ks[15], (L, LRU_BLOCKS, LRU_BLOCK, LRU_BLOCK), LRU_BLOCK ** -0.5)
    lru_b_i = nrm(ks[16], (L, LRU_WIDTH), 0.02)
    a0 = jax.random.uniform(ks[17], (L, LRU_WIDTH), f32, 0.9, 0.999)
    s0 = a0 ** (1.0 / LRU_C)
    lru_lambda = jnp.log(s0) - jnp.log1p(-s0)
    w_out_lru = nrm(ks[18], (L, LRU_WIDTH, D_MODEL), LRU_WIDTH ** -0.5)

    w_out = nrm(ks[19], (L, D_MODEL, D_MODEL), D_MODEL ** -0.5)
    norm2_w = 1.0 + nrm(ks[20], (L, D_MODEL), 0.05)
    w_ffn_in = nrm(ks[21], (L, D_MODEL, 2 * FFN_HIDDEN), D_MODEL ** -0.5)
    w_ffn_out = nrm(ks[22], (L, FFN_HIDDEN, D_MODEL), FFN_HIDDEN ** -0.5)
    norm_f_w = 1.0 + nrm(ks[23], (D_MODEL,), 0.05)

    return {
        "x": x, "norm1_w": norm1_w, "w_in": w_in, "b_branch_gate": b_branch_gate,
        "ssm_conv_w": ssm_conv_w, "ssm_conv_b": ssm_conv_b, "ssm_dt_bias": ssm_dt_bias,
        "ssm_a_log": ssm_a_log, "ssm_d": ssm_d, "ssm_norm_w": ssm_norm_w, "w_out_ssm": w_out_ssm,
        "lru_conv_w": lru_conv_w, "lru_conv_b": lru_conv_b, "lru_w_r": lru_w_r, "lru_b_r": lru_b_r,
        "lru_w_i": lru_w_i, "lru_b_i": lru_b_i, "lru_lambda": lru_lambda, "w_out_lru": w_out_lru,
        "w_out": w_out, "norm2_w": norm2_w, "w_ffn_in": w_ffn_in, "w_ffn_out": w_ffn_out,
        "norm_f_w": norm_f_w,
    }


def reference(x, norm1_w, w_in, b_branch_gate, ssm_conv_w, ssm_conv_b, ssm_dt_bias, ssm_a_log,
              ssm_d, ssm_norm_w, w_out_ssm, lru_conv_w, lru_conv_b, lru_w_r, lru_b_r, lru_w_i,
              lru_b_i, lru_lambda, w_out_lru, w_out, norm2_w, w_ffn_in, w_ffn_out, norm_f_w):
    h = x
    for l in range(DEPTH):
        hn = rmsnorm(h, norm1_w[l])
        proj = hn @ w_in[l]
        gates, z, xbc, dt_raw, lru_x, lru_y = jnp.split(proj, IN_SPLITS, axis=-1)
        gates = jax.nn.sigmoid(gates + b_branch_gate[l])
        g_ssm, g_lru = jnp.split(gates, 2, axis=-1)
        y_ssm = mamba2_mixer(z, xbc, dt_raw, ssm_conv_w[l], ssm_conv_b[l], ssm_dt_bias[l],
                             ssm_a_log[l], ssm_d[l], ssm_norm_w[l]) @ w_out_ssm[l]
        y_lru = rglru_mixer(lru_x, lru_y, lru_conv_w[l], lru_conv_b[l], lru_w_r[l], lru_b_r[l],
                            lru_w_i[l], lru_b_i[l], lru_lambda[l]) @ w_out_lru[l]
        h = h + (g_ssm * y_ssm + g_lru * y_lru) @ w_out[l]
        hn = rmsnorm(h, norm2_w[l])
        gate, up = jnp.split(hn @ w_ffn_in[l], 2, axis=-1)
        h = h + (jax.nn.silu(gate) * up) @ w_ffn_out[l]
    return rmsnorm(h, norm_f_w)
```

```python
import numpy as np
from contextlib import ExitStack
import concourse.bass as bass
import concourse.mybir as mybir
from concourse.bass_utils import run_bass_kernel_spmd

F32 = mybir.dt.float32
BF16 = mybir.dt.bfloat16
AF = mybir.ActivationFunctionType
ALU = mybir.AluOpType

D_MODEL = 1024
SEQ = 4096
BATCH = 4
TT_ = 512
NT_MAIN = 4
NT_PRE = 4
EPS = 1e-6
FFN_H = 2816
NPAR = 360


class Buf:
    __slots__ = ("name", "last_w", "readers")

    def __init__(self, name):
        self.name = name
        self.last_w = None
        self.readers = {}


class Op:
    __slots__ = ("eng", "fn", "is_dma", "deps", "needs_inc", "tick", "sem_key", "gid")


class Sched:
    ENGS = ("pe", "act", "dve", "pool", "sp")

    def __init__(self, nc):
        self.nc = nc
        self.ops = []
        self.rec = []
        self.aset = {}
        self.dma_count = {}
        self.group_keys = set()

    @staticmethod
    def _k(p):
        return ("d", p.sem_key) if p.is_dma else ("e", p.eng)

    def _add(self, eng, fn, reads, writes, is_dma=False, sem_key=None):
        op = Op()
        op.eng, op.fn, op.is_dma, op.sem_key = eng, fn, is_dma, sem_key
        op.needs_inc = False
        op.tick = None
        op.gid = len(self.ops)
        deps = {}

        def dep(p):
            if p is None:
                return
            k = self._k(p)
            q = deps.get(k)
            if q is None or q.gid < p.gid:
                deps[k] = p

        for b in reads:
            dep(b.last_w)
        for b in writes:
            dep(b.last_w)
            for r in b.readers.values():
                dep(r)
        op.deps = list(deps.values())
        for b in reads:
            b.readers[self._k(op) if not is_dma else ("d", sem_key)] = op
        for b in writes:
            b.last_w = op
            b.readers = {}
        if is_dma:
            self.dma_count[sem_key] = self.dma_count.get(sem_key, 0) + 16
            op.tick = self.dma_count[sem_key]
        self.ops.append(op)
        return op

    def op(self, eng, fn, reads=(), writes=(), cost=500.0):
        self.rec.append((eng, fn, list(reads), list(writes), False, None, float(cost)))
        return len(self.rec) - 1

    def dma(self, queue, fn, reads=(), writes=(), key=None, group=False, cost=6000.0):
        self.rec.append((queue, fn, list(reads), list(writes), True, key, float(cost)))
        return len(self.rec) - 1

    def schedule(self, window=600, lat=150.0, wmap=None, slack=0.0):
        wmap = wmap or {}
        rec = self.rec
        n = len(rec)
        last_w = {}
        rdrs = {}
        preds = [None] * n
        for i, (eng, fn, reads, writes, is_dma, key, cost) in enumerate(rec):
            p = set()
            for b in reads:
                w = last_w.get(id(b))
                if w is not None:
                    p.add(w)
            for b in writes:
                w = last_w.get(id(b))
                if w is not None:
                    p.add(w)
                p.update(rdrs.get(id(b), ()))
            p.discard(i)
            preds[i] = p
            for b in reads:
                rdrs.setdefault(id(b), []).append(i)
            for b in writes:
                last_w[id(b)] = i
                rdrs[id(b)] = []
        succs = [[] for _ in range(n)]
        indeg = [0] * n
        for i in range(n):
            indeg[i] = len(preds[i])
            for p in preds[i]:
                succs[p].append(i)
        import heapq
        blevel = [0.0] * n
        for i in range(n - 1, -1, -1):
            m_ = 0.0
            for sx in succs[i]:
                if blevel[sx] > m_:
                    m_ = blevel[sx]
            blevel[i] = m_ + rec[i][6]
        ready = [i for i in range(n) if indeg[i] == 0]
        heapq.heapify(ready)
        cand = []
        efree = {e: 0.0 for e in self.ENGS}
        cur_set = [None]
        n_switch = [0]

        def act_pen(i):
            need = self.aset.get(i)
            if need is None:
                return 0.0
            c = cur_set[0]
            ok = (need == "tanh" and c in ("E", "Si", "Ge")) or (need == "gelu" and c == "Ge") or (need == "exp" and c in ("E", "L")) or (need == "ln" and c == "L") or \
                 (need == "sqrt" and c == "S") or (need == "silu" and c == "Si")
            return 0.0 if ok else 2600.0
        finish = [0.0] * n
        done = [False] * n
        order = []
        base = 0
        while len(order) < n:
            while base < n and done[base]:
                base += 1
            while ready and ready[0] < base + window:
                cand.append(heapq.heappop(ready))
            best = None
            bt = None
            tl = []
            for i in cand:
                eng = rec[i][0]
                if i >= base + wmap.get(eng, window):
                    continue
                t = efree[eng]
                if eng == "act":
                    t += act_pen(i)
                for p in preds[i]:
                    f = finish[p] + lat
                    if f > t:
                        t = f
                tl.append((t, i))
                if bt is None or t < bt - 1e-9 or (abs(t - bt) <= 1e-9 and i < best):
                    best, bt = i, t
            if slack > 0.0:
                bb = None
                for (t, i) in tl:
                    if t <= bt + slack and (bb is None or blevel[i] > blevel[bb] or (blevel[i] == blevel[bb] and i < bb)):
                        bb = i
                for (t, i) in tl:
                    if i == bb:
                        best, bt = i, t
                        break
            i = best
            cand.remove(i)
            eng, fn, reads, writes, is_dma, key, cost = rec[i]
            if eng == "act" and i in self.aset:
                if act_pen(i) > 0.0:
                    n_switch[0] += 1
                    need = self.aset[i]
                    cur_set[0] = {"tanh": "E", "exp": "E", "ln": "L", "sqrt": "S", "silu": "Si", "gelu": "Ge"}[need]
            if is_dma:
                efree[eng] = bt + 100.0
                finish[i] = bt + cost
            else:
                efree[eng] = bt + cost
                finish[i] = bt + cost
            done[i] = True
            order.append(i)
            for sidx in succs[i]:
                indeg[sidx] -= 1
                if indeg[sidx] == 0:
                    heapq.heappush(ready, sidx)
        self.est_total = max(finish) if n else 0.0
        self.n_switch = n_switch[0]
        return order

    def replay(self, order):
        rec = self.rec
        seen = set()
        for tup in rec:
            for b in tup[2] + tup[3]:
                if id(b) not in seen:
                    seen.add(id(b))
                    b.last_w = None
                    b.readers = {}
        self.ops = []
        self.dma_count = {}
        m = {}
        for i in order:
            eng, fn, reads, writes, is_dma, key, cost = rec[i]
            m[i] = self._add(eng, fn, reads, writes, is_dma=is_dma, sem_key=key)
        return m

    def _skip(self, p, o):
        return (not p.is_dma) and (not o.is_dma) and p.eng == o.eng and p.eng == "pe"

    def emit(self, final_wait_ops=()):
        nc = self.nc
        ops = self.ops
        for o in ops:
            for p in o.deps:
                if p.is_dma or self._skip(p, o):
                    continue
                p.needs_inc = True
        ticks = {e: 0 for e in self.ENGS}
        for o in ops:
            if not o.is_dma and o.needs_inc:
                ticks[o.eng] += 1
                o.tick = ticks[o.eng]
        per_eng = {e: [o for o in ops if o.eng == e] for e in self.ENGS}
        with ExitStack() as st:
            esem = {e: st.enter_context(nc.semaphore("s_" + e)) for e in ("pe", "act", "dve", "pool")}
            dsem = {}
            for i, k in enumerate(self.dma_count):
                dsem[k] = st.enter_context(nc.semaphore("d%d" % i))
            block = st.enter_context(nc.Block())

            def run(ename, eng):
                waited = {}
                for o in per_eng[ename]:
                    for p in o.deps:
                        if p.is_dma:
                            s = dsem[p.sem_key]
                            v = self.dma_count[p.sem_key] if p.sem_key in self.group_keys else p.tick
                        else:
                            if self._skip(p, o):
                                continue
                            s = esem[p.eng]
                            v = p.tick
                        kk = self._k(p)
                        if waited.get(kk, 0) >= v:
                            continue
                        eng.wait_ge(s, v)
                        waited[kk] = v
                    inst = o.fn(eng)
                    if o.is_dma:
                        inst.then_inc(dsem[o.sem_key], 16)
                    elif o.needs_inc:
                        inst.then_inc(esem[o.eng], 1)
                if ename == "sp":
                    for p in final_wait_ops:
                        v = self.dma_count[p.sem_key] if p.sem_key in self.group_keys else p.tick
                        eng.wait_ge(dsem[p.sem_key], v)

            @block.tensor
            def _(e):
                run("pe", e)

            @block.scalar
            def _(e):
                run("act", e)

            @block.vector
            def _(e):
                run("dve", e)

            @block.gpsimd
            def _(e):
                run("pool", e)

            @block.sync
            def _(e):
                run("sp", e)


def build_program(debug=(), ntiles=NT_PRE + NT_MAIN, cut=99):
    import os
    SSDCUT = int(os.environ.get('SSDCUT', '99'))
    nc = bass.Bass("TRN2", target_bir_lowering=False)
    S = Sched(nc)

    def din(name, shape, dt=F32):
        return nc.dram_tensor(name, shape, dt, kind="ExternalInput").ap()

    def dscr(name, shape, dt=BF16):
        return nc.dram_tensor(name, shape, dt, kind="Internal").ap()

    xT_d = din("xT", [D_MODEL, (NT_PRE + NT_MAIN) * TT_])
    par_d = din("par", [128, NPAR])
    cf_d = din("cf32", [128, 384])
    cb_d = din("cbf", [128, 256], BF16)
    out_d = nc.dram_tensor("outT", [D_MODEL, NT_MAIN * TT_], F32, kind="ExternalOutput").ap()

    WSPEC = {"win": (20, 4096), "wdt": (1, 256), "wos": (4, 4096), "wol": (4, 2560), "wout": (2, 4096),
             "wfi": (11, 4096), "wfo": (8, 2816), "wri": (1, 2560)}
    w_d = {k: din(k, [g, 128, n]) for k, (g, n) in WSPEC.items()}
    w_s = {k: dscr(k + "_s", [g, 128, n]) for k, (g, n) in WSPEC.items()}

    dbg_out = {}

    with ExitStack() as st:
        def sb(name, shape, dt):
            return st.enter_context(nc.sbuf_tensor("sb_" + name, shape, dt))

        def ps(name, shape, dt):
            return st.enter_context(nc.psum_tensor("ps_" + name, shape, dt))

        par = sb("par", [128, NPAR], F32); b_par = Buf("par")
        cf = sb("cf", [128, 384], F32); b_cf = Buf("cf")
        cbf = sb("cbf", [128, 256], BF16); b_cbf = Buf("cbf")
        tri = cf[:, 0:128]
        Pm = cf[:, 128:256]
        onesf = cf[:, 256:384]
        ident = cbf[:, 0:128]
        onesb = cbf[:, 128:256]
        der = sb("der", [128, 256], F32); b_der = Buf("der")
        wdt = sb("wdt", [128, 8, 32], BF16); b_wdt = Buf("wdt")
        XH = [sb("XH%d" % i, [128, 8, TT_], F32) for i in range(2)]
        b_XH = [Buf("XH%d" % i) for i in range(2)]
        hnT = sb("hnT", [128, 8, TT_], BF16); b_hn = Buf("hnT")
        BIG = sb("BIG", [128, 24, TT_], BF16); b_BIG = [Buf("BIG%d" % i) for i in range(24)]
        ZS = sb("ZS", [128, 4, 2048], BF16); b_ZS = [Buf("ZS%d" % i) for i in range(4)]
        oT = ZS[:].rearrange("p a b -> p (a b)")[:, 0:10 * TT_].rearrange("p (a b) -> p a b", a=10)
        ynT = sb("ynT", [128, 16, TT_], BF16); b_yn = [Buf("ynT%d" % i) for i in range(4)]
        mT = sb("mT", [128, 8, TT_], BF16); b_mT = [Buf("mT%d" % i) for i in range(8)]
        NR = 4
        ring = [sb("ring%d" % i, [128, 4096], BF16) for i in range(NR)]
        b_ring = [Buf("ring%d" % i) for i in range(NR)]
        xs_tok = sb("xs_tok", [128, 2048], BF16); b_xs = Buf("xs_tok")
        xdt = sb("xdt", [128, 2048], BF16); b_xdt = Buf("xdt")
        xdd = sb("xdd", [128, 2048], BF16); b_xdd = Buf("xdd")
        B_tok = sb("B_tok", [128, 512], BF16); b_Bt = Buf("B_tok")
        cbTm = sb("cbTm", [128, 4, 128], BF16); b_cbT = Buf("cbTm")
        Rt = sb("Rt", [128, 8, 128], F32); b_R = [Buf("R0"), Buf("R1")]
        Lt = sb("Lt", [128, 8, 128], BF16); b_Lh = [Buf("L0"), Buf("L1")]
        Mt = sb("Mt", [128, 16, 128], BF16); b_M = [Buf("M%d" % i) for i in range(4)]
        Sst = sb("Sst", [128, 2048], F32); b_S = [Buf("S%d" % i) for i in range(4)]
        Sbf = sb("Sbf", [128, 2048], BF16); b_Sb = [Buf("Sb%d" % i) for i in range(4)]
        TMP = [sb("tmp%d" % i, [128, TT_], F32) for i in range(10)]
        b_T = [Buf("tmp%d" % i) for i in range(10)]
        ynb2 = [sb("ynb%d" % i, [128, 512], BF16) for i in range(2)]; b_ynb2 = [Buf("ynb0"), Buf("ynb1")]
        U = [sb("U%d" % i, [128, TT_ + 4], BF16) for i in range(2)]
        b_U = [Buf("U%d" % i) for i in range(2)]
        Hs = sb("Hs", [128, 24, 4], BF16); b_Hs = [Buf("Hs%d" % i) for i in range(24)]
        Hl = sb("Hl", [128, 10, 4], BF16); b_Hl = [Buf("Hl%d" % i) for i in range(10)]
        hlast = sb("hlast", [128, 10], F32); b_hl = [Buf("hl%d" % i) for i in range(10)]
        SM = sb("SM", [128, 1024], F32)
        dt_sb = SM[:, 0:128].rearrange("p (c h) -> p c h", c=4); b_dt = [Buf("dt%d" % i) for i in range(4)]
        dA_sb = SM[:, 128:256].rearrange("p (c h) -> p c h", c=4); b_dA = [Buf("dA%d" % i) for i in range(4)]
        cs_sb = SM[:, 256:320]; b_cs = Buf("cs")
        ecs_sb = SM[:, 320:352]; b_ecs = Buf("ecs")
        eend_sb = SM[:, 352:384]; b_eend = Buf("eend")
        dd_sb = SM[:, 384:416]; b_dd = Buf("dd")
        dtdec_sb = SM[:, 416:448]; b_dtdec = Buf("dtdec")
        sp1 = SM[:, 448:480]; b_sp1 = Buf("sp1")
        sp2 = SM[:, 480:512]; b_sp2 = Buf("sp2")
        sp3 = SM[:, 512:544]; b_sp3 = Buf("sp3")
        ssq2 = [SM[:, 544:545], SM[:, 548:549]]; b_ssq2 = [Buf("ssq0"), Buf("ssq1")]
        sdv2 = [SM[:, 545:546], SM[:, 549:550]]; b_sd2 = [Buf("sd0"), Buf("sd1")]
        rsg2 = [SM[:, 546:547], SM[:, 550:551]]; b_rsg2 = [Buf("rsg0"), Buf("rsg1")]

        G = [ps("G%d" % i, [128, 512], F32) for i in range(4)]
        b_G = [Buf("G%d" % i) for i in range(4)]
        Dp = ps("Dp", [128, 1024], F32); b_Dp = [Buf("Dp0"), Buf("Dp1")]
        TR = [ps("TR%d" % i, [128, 1024], BF16) for i in range(2)]
        b_TR = [Buf("TR%d" % i) for i in range(2)]
        gstate = {"g": 0, "tr": 0, "ring": 0}

        def nbank():
            i = gstate["g"] % 4
            gstate["g"] += 1
            return G[i], b_G[i]

        def ntr():
            i = gstate["tr"] % 2
            gstate["tr"] += 1
            return TR[i], b_TR[i]

        def nel(ap):
            n = 1
            for d in list(ap.shape)[1:]:
                n *= int(d)
            return n

        def PE(out, lhsT, rhs, start, stop, r, w):
            f = 4.0 if lhsT.dtype == F32 else 1.0
            S.op("pe", lambda e: e.matmul(out, lhsT=lhsT, rhs=rhs, start=start, stop=stop), r, w,
                 cost=30.0 + 0.55 * f * nel(rhs))

        def PET(out, in_, r, w):
            S.op("pe", lambda e: e.transpose(out, in_, ident), list(r) + [b_cbf], w, cost=260.0)

        def ACT(out, in_, func, r, w, bias=None, scale=None, accum_out=None):
            kw = {}
            if bias is not None:
                kw["bias"] = bias
            if scale is not None:
                kw["scale"] = scale
            if accum_out is not None:
                kw["accum_out"] = accum_out
            idx_ = S.op("act", lambda e: e.activation(out=out, in_=in_, func=func, **kw), r, w, cost=200.0 + 0.85 * nel(out))
            need_ = {AF.Tanh: "tanh", AF.Exp: "exp", AF.Ln: "ln", AF.Sqrt: "sqrt", AF.Silu: "silu", AF.Gelu_apprx_tanh: "gelu"}.get(func)
            if need_:
                S.aset[idx_] = need_

        def TTo(eng, out, in0, in1, op, r, w):
            S.op(eng, lambda e: e.tensor_tensor(out=out, in0=in0, in1=in1, op=op), r, w,
                 cost=(250.0 + 1.9 * nel(out)) if eng == "pool" else (150.0 + 1.05 * nel(out)))

        def TS(eng, out, in0, s1, s2, op0, op1, r, w):
            cst = (250.0 + 1.9 * nel(out)) if eng == "pool" else (150.0 + 1.05 * nel(out))
            if op1 is None:
                S.op(eng, lambda e: e.tensor_scalar(out=out, in0=in0, scalar1=s1, scalar2=None, op0=op0), r, w, cost=cst)
            else:
                S.op(eng, lambda e: e.tensor_scalar(out=out, in0=in0, scalar1=s1, scalar2=s2, op0=op0, op1=op1), r, w,
                     cost=cst)

        def STT(out, in0, scalar, in1, op0, op1, r, w):
            S.op("dve", lambda e: e.scalar_tensor_tensor(out=out, in0=in0, scalar=scalar, in1=in1, op0=op0, op1=op1), r, w,
                 cost=150.0 + 1.05 * nel(out))

        def CP(eng, out, in_, r, w):
            if eng == "act":
                ACT(out, in_, AF.Copy, r, w)
            else:
                S.op(eng, lambda e: e.tensor_copy(out=out, in_=in_), r, w,
                     cost=(250.0 + 1.9 * nel(out)) if eng == "pool" else (150.0 + 1.05 * nel(out)))

        def dump(name, ap, shape, dt, r):
            if name not in debug:
                return
            d = nc.dram_tensor("dbg_" + name, list(shape), dt, kind="ExternalOutput").ap()
            dbg_out[name] = S.dma("sp", lambda e: e.dma_start(out=d, in_=ap), reads=r, key=("dbg", name))

        cast_groups = {"c_in0": [("win", g) for g in range(0, 6)] + [("wdt", 0)],
                       "c_in1": [("win", g) for g in range(6, 12)],
                       "c_wos": [("wos", g) for g in range(4)],
                       "c_in2": [("win", g) for g in range(12, 15)] + [("wri", 0)],
                       "c_in3": [("win", g) for g in range(15, 20)],
                       "c_wol": [("wol", g) for g in range(4)],
                       "c_wout": [("wout", g) for g in range(2)],
                       "c_wfi": [("wfi", g) for g in range(11)],
                       "c_wfo": [("wfo", g) for g in range(8)]}
        b_scr = {}
        gran_buf = {}
        S.dma("sp", lambda e: e.dma_start(out=par[:], in_=par_d[:, :]), writes=[b_par], key="par")
        S.dma("sp", lambda e: e.dma_start(out=cf[:], in_=cf_d[:, :]), writes=[b_cf], key="cf")
        S.dma("sp", lambda e: e.dma_start(out=cbf[:], in_=cb_d[:, :]), writes=[b_cbf], key="cbf")
        pre_list = [("win", g) for g in range(0, 6)] + [("wdt", 0)] + [("win", g) for g in (12, 13, 14)] + [("wri", 0)]
        main_list = ([("win", g) for g in range(6, 12)] + [("wos", g) for g in range(4)] + [("win", g) for g in range(15, 20)]
                     + [("wol", g) for g in range(4)] + [("wout", g) for g in range(2)] + [("wfi", g) for g in range(11)]
                     + [("wfo", g) for g in range(8)])
        bigflat = BIG[:].rearrange("p a b -> p (a b)")
        cstate = {"n": 0}

        def conv_load(wn, g, xi):
            n = WSPEC[wn][1]
            xf = XH[xi][:].rearrange("p a b -> p (a b)")
            S.dma("sp", (lambda e, d=xf[:, 0:n], s_=w_d[wn][g]: e.dma_start(out=d, in_=s_)), writes=[b_XH[xi]],
                  key=("x", xi), cost=2000.0 + n * 512 / 200.0)

        def conv_cast_store(wn, g, xi, eng):
            n = WSPEC[wn][1]
            xf = XH[xi][:].rearrange("p a b -> p (a b)")
            j = cstate["n"] % 3
            cstate["n"] += 1
            dstb = bigflat[:, j * 4096:j * 4096 + n]
            bdst = b_BIG[8 * j:8 * j + 8]
            CP(eng, dstb, xf[:, 0:n], [b_XH[xi]], bdst)
            gran_buf[(wn, g)] = Buf("scr_%s_%d" % (wn, g))
            S.dma("sp", (lambda e, d=w_s[wn][g], s_=dstb: e.dma_start(out=d, in_=s_)), reads=bdst,
                  writes=[gran_buf[(wn, g)]], key=("sto", j), cost=2000.0 + n * 256 / 200.0)

        cast_engs = ["pool", "dve", "act"]
        conv_load(pre_list[0][0], pre_list[0][1], 0)
        for i in range(len(pre_list)):
            if i + 1 < len(pre_list):
                conv_load(pre_list[i + 1][0], pre_list[i + 1][1], (i + 1) % 2)
            conv_cast_store(pre_list[i][0], pre_list[i][1], i % 2, cast_engs[i % 3])

        def load_gran(wn, g):
            i = gstate["ring"] % NR
            gstate["ring"] += 1
            n = WSPEC[wn][1]
            t = ring[i]
            src = w_s[wn][g]
            S.dma("sp", lambda e: e.dma_start(out=t[:, 0:n], in_=src), reads=[gran_buf[(wn, g)]], writes=[b_ring[i]],
                  key=("ring", i), cost=2000.0 + n * 256 / 200.0)
            return t, b_ring[i]

        S.dma("sp", lambda e: e.dma_start(out=wdt[:].rearrange("p k c -> p (k c)"), in_=w_s["wdt"][0]),
              reads=[gran_buf[("wdt", 0)]], writes=[b_wdt], key="wdt")

        w1 = par[:, 0:8]; w2 = par[:, 8:16]; wf = par[:, 16:24]; bg = par[:, 24:40]
        cws = par[:, 40:136]; cbs = par[:, 136:160]; cwl = par[:, 160:200]; cbl = par[:, 200:210]
        br = par[:, 210:220]; bi = par[:, 220:230]; lam = par[:, 230:240]; nw = par[:, 240:256]
        dtb = par[:, 256:288]; alog = par[:, 288:320]; Drep = par[:, 320:352]; smask = par[:, 352:353]
        A_rep = der[:, 0:32]; ca = der[:, 32:42]; hca = der[:, 42:52]
        mhalf = der[:, 52:53]; hbr = der[:, 54:64]; hbi = der[:, 64:74]; hbg = der[:, 74:90]
        cws_h = der[:, 90:186]; cbs_h = der[:, 186:210]

        ACT(der[:, 0:32], alog, AF.Exp, [b_par], [b_der])
        TS("dve", der[:, 0:32], der[:, 0:32], -1.0, None, ALU.mult, None, [b_der], [b_der])
        ACT(sp1[:, 0:10], lam, AF.Abs, [b_par], [b_sp1])
        ACT(sp1[:, 0:10], sp1[:, 0:10], AF.Exp, [b_sp1], [b_sp1], scale=-1.0)
        ACT(sp1[:, 0:10], sp1[:, 0:10], AF.Ln, [b_sp1], [b_sp1], bias=1.0)
        TS("dve", sp2[:, 0:10], lam, -1.0, 0.0, ALU.mult, ALU.max, [b_par], [b_sp2])
        TTo("dve", sp1[:, 0:10], sp1[:, 0:10], sp2[:, 0:10], ALU.add, [b_sp1, b_sp2], [b_sp1])
        TS("dve", der[:, 32:42], sp1[:, 0:10], -8.0, None, ALU.mult, None, [b_sp1, b_der], [b_der])
        TS("dve", der[:, 42:52], sp1[:, 0:10], -4.0, None, ALU.mult, None, [b_sp1, b_der], [b_der])
        S.op("pool", lambda e: e.memset(der[:, 52:54], -0.5), [b_der], [b_der])
        TS("dve", der[:, 54:64], br, 0.5, None, ALU.mult, None, [b_par, b_der], [b_der])
        TS("dve", der[:, 64:74], bi, 0.5, None, ALU.mult, None, [b_par, b_der], [b_der])
        TS("dve", der[:, 74:90], bg, 0.5, None, ALU.mult, None, [b_par, b_der], [b_der])
        TS("dve", der[:, 90:186], cws, 0.5, None, ALU.mult, None, [b_par, b_der], [b_der])
        TS("dve", der[:, 186:210], cbs, 0.5, None, ALU.mult, None, [b_par, b_der], [b_der])
        S.op("pool", lambda e: e.memset(Sst[:], 0.0), [], b_S)
        S.op("pool", lambda e: e.memset(Sbf[:], 0.0), [], b_Sb)
        S.op("pool", lambda e: e.memset(Hs[:], 0.0), [], b_Hs)
        S.op("pool", lambda e: e.memset(Hl[:], 0.0), [], b_Hl)
        S.op("pool", lambda e: e.memset(hlast[:], 0.0), [], b_hl)

        def rmsnorm(xh, bxh, wcols, out_tile, bout, out_is_f32):
            for k in range(8):
                ACT(BIG[:, k, :], xh[:, k, :], AF.Square, [bxh], [b_BIG[k]])
            pb, bpb = nbank()
            for k in range(8):
                PE(pb[:, :], onesb, BIG[:, k, :], k == 0, k == 7, [b_BIG[k], b_cbf], [bpb])
            ACT(TMP[0][:], pb[:, :], AF.Ln, [bpb], [b_T[0]], bias=EPS, scale=1.0 / D_MODEL)
            ACT(TMP[1][:], TMP[0][:], AF.Exp, [b_T[0]], [b_T[1]], scale=-0.5)
            for k in range(8):
                STT(out_tile[:, k, :], xh[:, k, :], wcols[:, k:k + 1], TMP[1][:], ALU.mult, ALU.mult,
                    [bxh, b_T[1], b_par], [bout])

        def conv_block(pbank, bpbank, ui, H, bH, hb, cw_ap, cbias_ap, acc_i):
            u, bu = U[ui], b_U[ui]
            CP("act", u[:, 3:3 + TT_], pbank[:, :], [bpbank], [bu])
            CP("act", u[:, 0:3], H[:, hb, 0:3], [bH[hb]], [bu])
            acc, bacc = TMP[acc_i], b_T[acc_i]
            ACT(acc[:], pbank[:, :], AF.Identity, [bpbank, b_par, b_der], [bacc], bias=cbias_ap, scale=cw_ap[:, 3:4])
            for k in (2, 1, 0):
                STT(acc[:], u[:, k:k + TT_], cw_ap[:, k:k + 1], acc[:], ALU.mult, ALU.add, [bu, bacc, b_par, b_der], [bacc])
            CP("dve", H[:, hb, 0:3], u[:, TT_:TT_ + 3], [bu], [bH[hb]])
            return acc, bacc

        fin_ops = []
        for ti in range(ntiles):
            main = ti >= NT_PRE
            xh, bxh = XH[ti % 2], b_XH[ti % 2]
            t0 = ti * TT_
            src = xT_d.rearrange("(k p) t -> p k t", p=128)[:, :, t0:t0 + TT_]
            S.dma("sp", (lambda e, s=src, d=xh: e.dma_start(out=d[:], in_=s)), writes=[bxh], key=("x", ti % 2), cost=12500.0)
            if not main:
                zs_f32 = ZS[:].rearrange("p a b -> p (a b)").bitcast(F32)
                mt_flat = mT[:].rearrange("p a b -> p (a b)")
                per = (len(main_list) + NT_PRE - 1) // NT_PRE
                pending = list(main_list[ti * per:(ti + 1) * per])

                def conv_step():
                    if not pending:
                        return
                    wn_, g_ = pending.pop(0)
                    n_ = WSPEC[wn_][1]
                    S.dma("sp", (lambda e, d=zs_f32[:, 0:n_], s_=w_d[wn_][g_]: e.dma_start(out=d, in_=s_)), writes=b_ZS,
                          key="cvl", cost=2000.0 + n_ * 512 / 200.0)
                    CP("act", mt_flat[:, 0:n_], zs_f32[:, 0:n_], b_ZS, b_mT)
                    gran_buf[(wn_, g_)] = Buf("scr_%s_%d" % (wn_, g_))
                    S.dma("sp", (lambda e, d=w_s[wn_][g_], s_=mt_flat[:, 0:n_]: e.dma_start(out=d, in_=s_)), reads=b_mT,
                          writes=[gran_buf[(wn_, g_)]], key="cvs", cost=2000.0 + n_ * 256 / 200.0)
            else:
                pending = []

                def conv_step():
                    return
            rmsnorm(xh, bxh, w1, hnT, b_hn, False)
            if ti == NT_PRE:
                dump("hnT", hnT[:], [128, 8, TT_], BF16, [b_hn])

            if cut < 2:
                continue
            ngr = 6 if (main or ti == NT_PRE - 1) else 5
            blkc = 0
            for gi in range(ngr):
                wt, bw = load_gran("win", gi)
                w3 = wt[:].rearrange("p (k c) -> p k c", k=8)
                for bl in range(4):
                    cb = gi * 4 + bl
                    pb, bpb = nbank()
                    for k in range(8):
                        PE(pb[:, :], w3[:, k, bl * 128:(bl + 1) * 128], hnT[:, k, :], k == 0, k == 7, [bw, b_hn], [bpb])
                    acc, bacc = conv_block(pb, bpb, blkc % 2, Hs, b_Hs, cb, cws[:, cb * 4:cb * 4 + 4], cbs[:, cb:cb + 1],
                                           2 + blkc % 2)
                    ACT(BIG[:, cb, :], acc[:], AF.Silu, [bacc], [b_BIG[cb]])
                    blkc += 1
                conv_step()
            if ti == NT_PRE:
                dump("xbcT", BIG[:], [128, 24, TT_], BF16, b_BIG)

            if cut < 3:
                continue
            for c in range(4):
                pb, bpb = nbank()
                for k in range(8):
                    PE(pb[:, 0:32], hnT[:, k, c * 128:(c + 1) * 128], wdt[:, k, :], k == 0, k == 7, [b_hn, b_wdt], [bpb])
                TTo("dve", dt_sb[:, c, :], pb[:, 0:32], dtb, ALU.add, [bpb, b_par], [b_dt[c]])
            dtf = SM[:, 0:128]
            dAf = SM[:, 128:256]
            t4 = TMP[4][:, 0:128]
            t5 = TMP[5][:, 0:128]
            ACT(t4, dtf, AF.Abs, b_dt, [b_T[4]])
            ACT(t4, t4, AF.Exp, [b_T[4]], [b_T[4]], scale=-1.0)
            ACT(t4, t4, AF.Ln, [b_T[4]], [b_T[4]], bias=1.0)
            TS("dve", t5, dtf, 0.0, None, ALU.max, None, b_dt, [b_T[5]])
            TTo("dve", dtf, t4, t5, ALU.add, [b_T[4], b_T[5]], b_dt)
            TTo("pool", dAf.rearrange("p (c h) -> p c h", c=4), dtf.rearrange("p (c h) -> p c h", c=4),
                A_rep.unsqueeze(1).broadcast_to([128, 4, 32]), ALU.mult, b_dt + [b_der], b_dA)
            if main:
                for g in range(4):
                    wt, bw = load_gran("win", 6 + g)
                    w3 = wt[:].rearrange("p (k c) -> p k c", k=8)
                    for c in range(4):
                        pb, bpb = nbank()
                        for k in range(8):
                            PE(pb[:, :], hnT[:, k, c * 128:(c + 1) * 128], w3[:, k, :], k == 0, k == 7, [b_hn, bw], [bpb])
                        ACT(ZS[:, c, g * 512:(g + 1) * 512], pb[:, :], AF.Silu, [bpb], [b_ZS[c]])

            if cut < 4:
                continue
            for c in range(4):
                csl = slice(c * 128, (c + 1) * 128)
                for rnd in range(2):
                    tr, btr = ntr()
                    for j in range(8):
                        blk = rnd * 8 + j
                        PET(tr[:, j * 128:(j + 1) * 128], BIG[:, blk, csl], [b_BIG[blk]], [btr])
                    CP("act", xs_tok[:, rnd * 1024:(rnd + 1) * 1024], tr[:, :], [btr], [b_xs])
                if main:
                    TTo("dve", xdt[:].rearrange("p (h q) -> p h q", h=32), xs_tok[:].rearrange("p (h q) -> p h q", h=32),
                        dt_sb[:, c, :].unsqueeze(2).broadcast_to([128, 32, 64]), ALU.mult, [b_xs, b_dt[c]], [b_xdt])
                tr, btr = ntr()
                for j in range(4):
                    PET(tr[:, j * 128:(j + 1) * 128], BIG[:, 16 + j, csl], [b_BIG[16 + j]], [btr])
                CP("act", B_tok[:, :], tr[:, 0:512], [btr], [b_Bt])
                pb, bpb = nbank()
                PE(pb[:, 0:32], tri, dA_sb[:, c, :], True, True, [b_cf, b_dA[c]], [bpb])
                PE(pb[:, 32:64], onesf, dA_sb[:, c, :], True, True, [b_cf, b_dA[c]], [bpb])
                CP("dve", cs_sb, pb[:, 0:64], [bpb], [b_cs])
                ACT(eend_sb, cs_sb[:, 32:64], AF.Exp, [b_cs], [b_eend])
                TTo("dve", dd_sb, cs_sb[:, 32:64], cs_sb[:, 0:32], ALU.subtract, [b_cs], [b_dd])
                ACT(dd_sb, dd_sb, AF.Exp, [b_dd], [b_dd])
                TTo("dve", dtdec_sb, dd_sb, dt_sb[:, c, :], ALU.mult, [b_dd, b_dt[c]], [b_dtdec])
                TTo("pool", xdd[:].rearrange("p (h q) -> p h q", h=32), xs_tok[:].rearrange("p (h q) -> p h q", h=32),
                    dtdec_sb.unsqueeze(2).broadcast_to([128, 32, 64]), ALU.mult, [b_xs, b_dtdec], [b_xdd])
                if main and SSDCUT >= 2:
                    ACT(ecs_sb, cs_sb[:, 0:32], AF.Exp, [b_cs], [b_ecs])
                    pcb, bpcb = nbank()
                    for g in range(4):
                        PE(pcb[:, g * 128:(g + 1) * 128], BIG[:, 16 + g, csl], BIG[:, 20 + g, csl], True, True,
                           [b_BIG[16 + g], b_BIG[20 + g]], [bpcb])
                    for g in range(4):
                        TTo("dve", cbTm[:, g, :], pcb[:, g * 128:(g + 1) * 128], tri, ALU.mult, [bpcb, b_cf], [b_cbT])
                    v8 = lambda ap: ap.rearrange("p (h q) -> p h q", h=8)

                    def ssd_A(hg):
                        g, hh = hg // 2, hg % 2
                        h0 = 8 * g + 4 * hh
                        Rh = Rt[:, 4 * hh:4 * hh + 4, :]
                        TTo("pool", Rh, dA_sb[:, c, h0:h0 + 4].unsqueeze(2).broadcast_to([128, 4, 128]),
                            tri.unsqueeze(1).broadcast_to([128, 4, 128]), ALU.mult, [b_dA[c], b_cf], [b_R[hh]])
                        Dh = Dp[:, hh * 512:(hh + 1) * 512]
                        PE(Dh, Pm, Rh.rearrange("p a b -> p (a b)"), True, True, [b_cf, b_R[hh]], [b_Dp[hh]])
                        Lh = Lt[:, 4 * hh:4 * hh + 4, :]
                        ACT(Lh.rearrange("p a b -> p (a b)"), Dh, AF.Exp, [b_Dp[hh]], [b_Lh[hh]])
                        ms = hg % 4
                        TTo("dve", Mt[:, 4 * ms:4 * ms + 4, :], Lh, cbTm[:, g, :].unsqueeze(1).broadcast_to([128, 4, 128]),
                            ALU.mult, [b_Lh[hh], b_cbT], [b_M[ms]])

                    def ssd_B(g):
                        gsl = slice(g * 512, (g + 1) * 512)
                        pq = g % 2
                        TA, TB = 4 + 2 * pq, 5 + 2 * pq
                        pyo, bpyo = nbank()
                        PE(pyo[:, :], BIG[:, 20 + g, csl], Sbf[:, gsl], True, True, [b_BIG[20 + g], b_Sb[g]], [bpyo])
                        pyd, bpyd = nbank()
                        for r in range(8):
                            h = 8 * g + r
                            ms = (2 * g + r // 4) % 4
                            PE(pyd[:, r * 64:(r + 1) * 64], Mt[:, 4 * ms + r % 4, :], xdt[:, h * 64:(h + 1) * 64], True, True,
                               [b_M[ms], b_xdt], [bpyd])
                        TTo("dve", v8(TMP[TA][:]), v8(pyo[:, :]),
                            ecs_sb[:, 8 * g:8 * g + 8].unsqueeze(2).broadcast_to([128, 8, 64]), ALU.mult,
                            [bpyo, b_ecs], [b_T[TA]])
                        TTo("pool", v8(TMP[TB][:]), v8(xs_tok[:, gsl]),
                            Drep[:, 8 * g:8 * g + 8].unsqueeze(2).broadcast_to([128, 8, 64]), ALU.mult,
                            [b_xs, b_par], [b_T[TB]])
                        TTo("pool", TMP[TB][:], TMP[TB][:], TMP[TA][:], ALU.add, [b_T[TA], b_T[TB]], [b_T[TB]])
                        TTo("dve", TMP[TB][:], pyd[:, :], TMP[TB][:], ALU.add, [bpyd, b_T[TB]], [b_T[TB]])
                        TTo("dve", TMP[TA][:], TMP[TB][:], ZS[:, c, gsl], ALU.mult, [b_T[TB], b_ZS[c]], [b_T[TA]])
                        ACT(TMP[TB][:], TMP[TA][:], AF.Square, [b_T[TA]], [b_T[TB], b_ssq2[pq]], accum_out=ssq2[pq])
                        TS("dve", sdv2[pq], ssq2[pq], 1.0 / 512, EPS, ALU.mult, ALU.add, [b_ssq2[pq]], [b_sd2[pq]])
                        S.op("pool", lambda e: e.tensor_tensor(out=rsg2[pq], in0=sdv2[pq], in1=mhalf, op=ALU.pow),
                             [b_sd2[pq], b_der], [b_rsg2[pq]])
                        ACT(ynb2[pq][:, :], TMP[TA][:], AF.Copy, [b_T[TA], b_rsg2[pq]], [b_ynb2[pq]], scale=rsg2[pq])
                        tr, btr = ntr()
                        for j in range(4):
                            PET(tr[:, j * 128:(j + 1) * 128], ynb2[pq][:, j * 128:(j + 1) * 128], [b_ynb2[pq]], [btr])
                        for j in range(4):
                            ACT(ynT[:, 4 * g + j, csl], tr[:, j * 128:(j + 1) * 128], AF.Copy, [btr, b_par], [b_yn[g]],
                                scale=nw[:, 4 * g + j:4 * g + j + 1])

                    ssd_A(0); ssd_A(1); ssd_A(2); ssd_A(3)
                    ssd_B(0)
                    ssd_A(4); ssd_A(5)
                    ssd_B(1)
                    ssd_A(6); ssd_A(7)
                    ssd_B(2)
                    ssd_B(3)
                for g in range(4):
                    gsl = slice(g * 512, (g + 1) * 512)
                    pst, bpst = nbank()
                    PE(pst[:, :], B_tok[:, g * 128:(g + 1) * 128], xdd[:, gsl], True, True, [b_Bt, b_xdd], [bpst])
                    v8 = lambda ap: ap.rearrange("p (h q) -> p h q", h=8)
                    TTo("pool", v8(TMP[9][:]), v8(Sst[:, gsl]),
                        eend_sb[:, 8 * g:8 * g + 8].unsqueeze(2).broadcast_to([128, 8, 64]), ALU.mult,
                        [b_S[g], b_eend], [b_T[9]])
                    TTo("dve", Sst[:, gsl], pst[:, :], TMP[9][:], ALU.add, [bpst, b_T[9]], [b_S[g]])
                    if not (ti == NT_PRE - 1 and c == 3):
                        CP("act", Sbf[:, gsl], Sst[:, gsl], [b_S[g]], [b_Sb[g]])
            if ti == NT_PRE - 1:
                for g in range(4):
                    gsl = slice(g * 512, (g + 1) * 512)
                    TS("dve", Sst[:, gsl], Sst[:, gsl], smask, None, ALU.mult, None, [b_S[g], b_par], [b_S[g]])
                    CP("act", Sbf[:, gsl], Sst[:, gsl], [b_S[g]], [b_Sb[g]])
            if ti == NT_PRE:
                dump("ynT", ynT[:], [128, 16, TT_], BF16, b_yn)
                dump("S", Sst[:], [128, 2048], F32, b_S)

            if cut < 5:
                continue
            if main:
                wgs = {}
                wos = {}
                for j in range(8):
                    if j % 4 == 0:
                        wgs = load_gran("win", 10 + j // 4)
                    if j % 2 == 0:
                        wos = load_gran("wos", j // 2)
                    wo3 = wos[0][:].rearrange("p (k c) -> p k c", k=16)
                    wg3 = wgs[0][:].rearrange("p (k c) -> p k c", k=8)
                    pa, bpa = nbank()
                    for k in range(16):
                        PE(pa[:, :], wo3[:, k, (j % 2) * 128:(j % 2 + 1) * 128], ynT[:, k, :], k == 0, k == 15,
                           [wos[1], b_yn[k // 4]], [bpa])
                    pg, bpg = nbank()
                    for k in range(8):
                        PE(pg[:, :], wg3[:, k, (j % 4) * 128:(j % 4 + 1) * 128], hnT[:, k, :], k == 0, k == 7,
                           [wgs[1], b_hn], [bpg])
                    si = 4 + j % 2
                    ACT(TMP[si][:], pg[:, :], AF.Tanh, [bpg, b_der], [b_T[si]], bias=hbg[:, j:j + 1], scale=0.5)
                    STT(mT[:, j, :], TMP[si][:], 1.0, pa[:, :], ALU.add, ALU.mult, [bpa, b_T[si]], [b_mT[j]])
                if ti == NT_PRE:
                    dump("mssm", mT[:], [128, 8, TT_], BF16, b_mT)

            if cut < 6:
                continue
            wri_flat = ynT[:].rearrange("p a b -> p (a b)")[:, 0:2560]
            S.dma("sp", lambda e: e.dma_start(out=wri_flat, in_=w_s["wri"][0]), reads=[gran_buf[("wri", 0)]], writes=b_yn,
                  key="wri")
            wri3 = wri_flat.rearrange("p (a h j) -> p a h j", a=2, h=10)
            b_wri = b_yn
            wlx = wly = None
            lx_pre = None
            if not main:
                lx_pre = [load_gran("win", 12 + i) for i in range(3)]
            Ltf = Lt[:].rearrange("p a b -> p (a b)")

            def lru_s1(bl, sl):
                Tu, Ta, To, Ti = 2 + sl, 4 + 3 * sl, 5 + 3 * sl, 6 + 3 * sl
                wx3 = wlx[0][:].rearrange("p (k c) -> p k c", k=8)
                pb, bpb = nbank()
                for k in range(8):
                    PE(pb[:, :], wx3[:, k, (bl % 4) * 128:(bl % 4 + 1) * 128], hnT[:, k, :], k == 0, k == 7,
                       [wlx[1], b_hn], [bpb])
                u, bu = conv_block(pb, bpb, sl, Hl, b_Hl, bl, cwl[:, bl * 4:bl * 4 + 4], cbl[:, bl:bl + 1], Tu)
                ubf = Ltf[:, sl * TT_:(sl + 1) * TT_]
                CP("act", ubf, u[:], [bu], [b_Lh[sl]])
                pr, bpr = nbank()
                PE(pr[:, :], wri3[:, 0, bl, :], ubf, True, True, b_wri + [b_Lh[sl]], [bpr])
                pi_, bpi = nbank()
                PE(pi_[:, :], wri3[:, 1, bl, :], ubf, True, True, b_wri + [b_Lh[sl]], [bpi])
                ACT(TMP[Ta][:], pr[:, :], AF.Tanh, [bpr, b_der], [b_T[Ta]], bias=hbr[:, bl:bl + 1], scale=0.5)
                ACT(TMP[To][:], TMP[Ta][:], AF.Exp, [b_T[Ta], b_der], [b_T[To]], bias=ca[:, bl:bl + 1], scale=ca[:, bl:bl + 1])
                ACT(TMP[Ta][:], TMP[Ta][:], AF.Exp, [b_T[Ta], b_der], [b_T[Ta]], bias=hca[:, bl:bl + 1],
                    scale=hca[:, bl:bl + 1])
                ACT(TMP[To][:], TMP[To][:], AF.Relu, [b_T[To]], [b_T[To]], bias=1.0, scale=-1.0)
                ACT(TMP[Ti][:], pi_[:, :], AF.Tanh, [bpi, b_der], [b_T[Ti]], bias=hbi[:, bl:bl + 1], scale=0.5)
                STT(TMP[Ti][:], TMP[Ti][:], 1.0, u[:], ALU.add, ALU.mult, [b_T[Ti], bu], [b_T[Ti]])

            def lru_sq(sl):
                To = 5 + 3 * sl
                ACT(TMP[To][:], TMP[To][:], AF.Sqrt, [b_T[To]], [b_T[To]], bias=1e-30, scale=1.0)

            def lru_s2(bl, sl):
                Tu, Ta, To, Ti = 2 + sl, 4 + 3 * sl, 5 + 3 * sl, 6 + 3 * sl
                STT(TMP[Ti][:], TMP[Ti][:], 0.5, TMP[To][:], ALU.mult, ALU.mult, [b_T[Ti], b_T[To]], [b_T[Ti]])
                S.op("dve", (lambda e: e.tensor_tensor_scan(out=TMP[Tu][:], data0=TMP[Ta][:], data1=TMP[Ti][:],
                                                            initial=hlast[:, bl:bl + 1], op0=ALU.mult, op1=ALU.add)),
                     [b_T[Ta], b_T[Ti], b_hl[bl]], [b_T[Tu]], cost=1250.0)
                CP("dve", hlast[:, bl:bl + 1], TMP[Tu][:, TT_ - 1:TT_], [b_T[Tu]], [b_hl[bl]])
                if main:
                    wy3 = wly[0][:].rearrange("p (k c) -> p k c", k=8)
                    py, bpy = nbank()
                    for k in range(8):
                        PE(py[:, :], wy3[:, k, (bl % 4) * 128:(bl % 4 + 1) * 128], hnT[:, k, :], k == 0, k == 7,
                           [wly[1], b_hn], [bpy])
                    ACT(TMP[To][:], py[:, :], AF.Gelu_apprx_tanh, [bpy], [b_T[To]])
                    TTo("dve", oT[:, bl, :], TMP[Tu][:], TMP[To][:], ALU.mult, [b_T[Tu], b_T[To]], b_ZS)

            for pr_ in range(5):
                if (2 * pr_) % 4 == 0:
                    if main:
                        wlx = load_gran("win", 12 + (2 * pr_) // 4)
                        wly = load_gran("win", 15 + (2 * pr_) // 4)
                    else:
                        wlx = lx_pre[(2 * pr_) // 4]
                lru_s1(2 * pr_, 0)
                lru_s1(2 * pr_ + 1, 1)
                lru_sq(0)
                lru_sq(1)
                lru_s2(2 * pr_, 0)
                lru_s2(2 * pr_ + 1, 1)
                conv_step()
            while pending:
                conv_step()
            if ti == NT_PRE - 1:
                TS("dve", hlast[:], hlast[:], smask, None, ALU.mult, None, b_hl + [b_par], b_hl)
            if ti == NT_PRE:
                dump("oT", oT, [128, 10, TT_], BF16, b_ZS)

            if not main or cut < 7:
                continue
            wgl = wol = None
            for j in range(8):
                if j % 4 == 0:
                    wgl = load_gran("win", 18 + j // 4)
                if j % 2 == 0:
                    wol = load_gran("wol", j // 2)
                wo3 = wol[0][:, 0:2560].rearrange("p (k c) -> p k c", k=10)
                wg3 = wgl[0][:].rearrange("p (k c) -> p k c", k=8)
                pa, bpa = nbank()
                for k in range(10):
                    PE(pa[:, :], wo3[:, k, (j % 2) * 128:(j % 2 + 1) * 128], oT[:, k, :], k == 0, k == 9,
                       [wol[1]] + b_ZS, [bpa])
                pg, bpg = nbank()
                for k in range(8):
                    PE(pg[:, :], wg3[:, k, (j % 4) * 128:(j % 4 + 1) * 128], hnT[:, k, :], k == 0, k == 7,
                       [wgl[1], b_hn], [bpg])
                si = 4 + j % 2
                ACT(TMP[si][:], pg[:, :], AF.Tanh, [bpg, b_der], [b_T[si]], bias=hbg[:, 8 + j:9 + j], scale=0.5)
                STT(TMP[6 + j % 2][:], TMP[si][:], 1.0, pa[:, :], ALU.add, ALU.mult, [bpa, b_T[si]], [b_T[6 + j % 2]])
                TTo("dve", mT[:, j, :], TMP[6 + j % 2][:], mT[:, j, :], ALU.add, [b_T[6 + j % 2], b_mT[j]], [b_mT[j]])
            if ti == NT_PRE:
                dump("mT", mT[:], [128, 8, TT_], BF16, b_mT)
            wo = None
            for j in range(8):
                if j % 4 == 0:
                    wo = load_gran("wout", j // 4)
                w3 = wo[0][:].rearrange("p (k c) -> p k c", k=8)
                pa, bpa = nbank()
                for k in range(8):
                    PE(pa[:, :], w3[:, k, (j % 4) * 128:(j % 4 + 1) * 128], mT[:, k, :], k == 0, k == 7,
                       [wo[1], b_mT[k]], [bpa])
                STT(xh[:, j, :], pa[:, :], 0.5, xh[:, j, :], ALU.mult, ALU.add, [bpa, bxh], [bxh])
            if ti == NT_PRE:
                dump("h1", xh[:], [128, 8, TT_], F32, [bxh])
            if cut < 8:
                continue
            rmsnorm(xh, bxh, w2, hnT, b_hn, False)
            for gi in range(11):
                wt, bw = load_gran("wfi", gi)
                w3 = wt[:].rearrange("p (k c) -> p k c", k=8)
                for q in range(2):
                    jb = 2 * gi + q
                    pa, bpa = nbank()
                    for k in range(8):
                        PE(pa[:, :], w3[:, k, q * 128:(q + 1) * 128], hnT[:, k, :], k == 0, k == 7, [bw, b_hn], [bpa])
                    pu, bpu = nbank()
                    for k in range(8):
                        PE(pu[:, :], w3[:, k, 256 + q * 128:256 + (q + 1) * 128], hnT[:, k, :], k == 0, k == 7,
                           [bw, b_hn], [bpu])
                    si = 4 + jb % 2
                    ACT(TMP[si][:], pa[:, :], AF.Silu, [bpa], [b_T[si]])
                    TTo("dve", BIG[:, jb, :], pu[:, :], TMP[si][:], ALU.mult, [bpu, b_T[si]], [b_BIG[jb]])
            for j in range(8):
                wt, bw = load_gran("wfo", j)
                w3 = wt[:, 0:2816].rearrange("p (k c) -> p k c", k=22)
                pa, bpa = nbank()
                for k in range(22):
                    PE(pa[:, :], w3[:, k, :], BIG[:, k, :], k == 0, k == 21, [bw, b_BIG[k]], [bpa])
                TTo("dve", xh[:, j, :], pa[:, :], xh[:, j, :], ALU.add, [bpa, bxh], [bxh])
            rmsnorm(xh, bxh, wf, xh, bxh, True)
            o0 = (ti - NT_PRE) * TT_
            dst = out_d.rearrange("(k p) t -> p k t", p=128)[:, :, o0:o0 + TT_]
            fo = S.dma("sp", (lambda e, s=xh, d=dst: e.dma_start(out=d, in_=s[:])), reads=[bxh], key=("out", ti % 2),
                       cost=12500.0)
            fin_ops.append(fo)

        if os.environ.get("NO_RESCHED"):
            order = list(range(len(S.rec)))
        else:
            _w = int(os.environ.get('SWIN', '600'))
            _wm = {'sp': 64}
            _wm.update({e: int(os.environ['SWIN_' + e.upper()]) for e in Sched.ENGS if ('SWIN_' + e.upper()) in os.environ})
            order = S.schedule(window=_w, wmap=_wm, slack=float(os.environ.get('SSLACK', '50')), lat=float(os.environ.get('SLAT', '100')))
        m = S.replay(order)
        if os.environ.get('SCHED_VERBOSE'):
            _tot = {}
            for _r in S.rec:
                _tot[_r[0]] = _tot.get(_r[0], 0.0) + (_r[6] if not _r[4] else 100.0)
            print('sched: per-engine model busy us', {k: round(v / 1e3) for k, v in _tot.items()})
            print('sched: act switches(model)', getattr(S, 'n_switch', -1), 'ops', len(S.rec), 'est_total_us', getattr(S, 'est_total', 0) / 1e3, 'moved', sum(1 for k, v in enumerate(order) if k != v))
        S.emit(final_wait_ops=[m[i] for i in fin_ops + list(dbg_out.values())])
    return nc


def _gran(W, nk, cw):
    K, C = W.shape
    G = C // cw
    assert K == nk * 128 and G * cw == C
    return np.ascontiguousarray(W.reshape(nk, 128, G, cw).transpose(2, 1, 0, 3).reshape(G, 128, nk * cw))


def _pp(v, nb):
    return np.ascontiguousarray(np.asarray(v, np.float32).reshape(nb, 128).T)


def prepare_inputs(inp):
    import ml_dtypes
    f = lambda a: np.asarray(a, np.float32)
    w_in = f(inp["w_in"])[0]
    gates_w, z_w, xbc_w = w_in[:, 0:2048], w_in[:, 2048:4096], w_in[:, 4096:7168]
    dt_w, lx_w, ly_w = w_in[:, 7168:7200], w_in[:, 7200:8480], w_in[:, 8480:9760]
    pad = np.zeros((1024, 256), np.float32)
    win_perm = np.concatenate([xbc_w, z_w, gates_w[:, :1024], lx_w, pad, ly_w, pad, gates_w[:, 1024:]], axis=1)
    assert win_perm.shape[1] == 10240
    W = {}
    W["win"] = _gran(win_perm, 8, 512)
    W["wdt"] = _gran(dt_w, 8, 32)
    W["wos"] = _gran(f(inp["w_out_ssm"])[0], 16, 256)
    W["wol"] = _gran(f(inp["w_out_lru"])[0], 10, 256)
    W["wout"] = _gran(f(inp["w_out"])[0], 8, 512)
    wfi = f(inp["w_ffn_in"])[0]
    gate_w, up_w = wfi[:, :FFN_H], wfi[:, FFN_H:]
    cols = []
    for gi in range(11):
        cols.append(gate_w[:, gi * 256:(gi + 1) * 256])
        cols.append(up_w[:, gi * 256:(gi + 1) * 256])
    W["wfi"] = _gran(np.concatenate(cols, axis=1), 8, 512)
    W["wfo"] = _gran(f(inp["w_ffn_out"])[0], 22, 128)
    wr = f(inp["lru_w_r"])[0].transpose(1, 0, 2).reshape(128, 1280)
    wi = f(inp["lru_w_i"])[0].transpose(1, 0, 2).reshape(128, 1280)
    W["wri"] = np.ascontiguousarray(np.concatenate([wr, wi], axis=1)[None])

    par = np.zeros((128, NPAR), np.float32)
    par[:, 0:8] = _pp(inp["norm1_w"][0], 8)
    par[:, 8:16] = _pp(inp["norm2_w"][0], 8)
    par[:, 16:24] = _pp(inp["norm_f_w"], 8)
    par[:, 24:40] = _pp(inp["b_branch_gate"][0], 16)
    cw = f(inp["ssm_conv_w"])[0]
    par[:, 40:136] = cw.reshape(4, 24, 128).transpose(2, 1, 0).reshape(128, 96)
    par[:, 136:160] = _pp(inp["ssm_conv_b"][0], 24)
    cl = f(inp["lru_conv_w"])[0]
    par[:, 160:200] = cl.reshape(4, 10, 128).transpose(2, 1, 0).reshape(128, 40)
    par[:, 200:210] = _pp(inp["lru_conv_b"][0], 10)
    par[:, 210:220] = _pp(inp["lru_b_r"][0], 10)
    par[:, 220:230] = _pp(inp["lru_b_i"][0], 10)
    par[:, 230:240] = _pp(inp["lru_lambda"][0], 10)
    par[:, 240:256] = _pp(inp["ssm_norm_w"][0], 16)
    par[:, 256:288] = f(inp["ssm_dt_bias"])[0][None, :]
    par[:, 288:320] = f(inp["ssm_a_log"])[0][None, :]
    par[:, 320:352] = f(inp["ssm_d"])[0][None, :]

    kk = np.arange(128)
    tri = (kk[:, None] <= kk[None, :]).astype(np.float32)
    Pm = (kk[:, None] > kk[None, :]).astype(np.float32)
    cf = np.concatenate([tri, Pm, np.ones((128, 128), np.float32)], axis=1)
    cb = np.concatenate([np.eye(128, dtype=np.float32), np.ones((128, 128), np.float32)], axis=1).astype(ml_dtypes.bfloat16)

    x = f(inp["x"])
    in_maps = []
    for c in range(8):
        b, half = c // 2, c % 2
        xT = np.zeros((D_MODEL, 4096), np.float32)
        if half == 1:
            xT[:, 0:2048] = x[b, 0:2048].T
        xT[:, 2048:4096] = x[b, half * 2048:(half + 1) * 2048].T
        p = par.copy()
        p[:, 352] = float(half)
        m = {"xT": xT, "par": p, "cf32": cf, "cbf": cb}
        m.update(W)
        in_maps.append(m)
    return in_maps


_NC_CACHE = {}


def kernel(**inputs):
    in_maps = prepare_inputs(inputs)
    if "nc" not in _NC_CACHE:
        _NC_CACHE["nc"] = build_program()
    nc = _NC_CACHE["nc"]
    res = run_bass_kernel_spmd(nc, in_maps, core_ids=list(range(8)))
    out = np.zeros((BATCH, SEQ, D_MODEL), np.float32)
    for c in range(8):
        b, half = c // 2, c % 2
        out[b, half * 2048:(half + 1) * 2048, :] = np.asarray(res.results[c]["outT"], np.float32).T
    return out
```

```python
import numpy as np
from contextlib import ExitStack
import concourse.bass as bass
import concourse.mybir as mybir
from concourse.bass_utils import run_bass_kernel_spmd

F32 = mybir.dt.float32
BF16 = mybir.dt.bfloat16
AF = mybir.ActivationFunctionType
ALU = mybir.AluOpType

D_MODEL = 1024
SEQ = 4096
BATCH = 4
TT_ = 512
NT_MAIN = 4
NT_PRE = 4
EPS = 1e-6
FFN_H = 2816
NPAR = 360


class Buf:
    __slots__ = ("name", "last_w", "readers")

    def __init__(self, name):
        self.name = name
        self.last_w = None
        self.readers = {}


class Op:
    __slots__ = ("eng", "fn", "is_dma", "deps", "needs_inc", "tick", "sem_key", "gid")


class Sched:
    ENGS = ("pe", "act", "dve", "pool", "sp")

    def __init__(self, nc):
        self.nc = nc
        self.ops = []
        self.rec = []
        self.aset = {}
        self.dma_count = {}
        self.group_keys = set()

    @staticmethod
    def _k(p):
        return ("d", p.sem_key) if p.is_dma else ("e", p.eng)

    def _add(self, eng, fn, reads, writes, is_dma=False, sem_key=None):
        op = Op()
        op.eng, op.fn, op.is_dma, op.sem_key = eng, fn, is_dma, sem_key
        op.needs_inc = False
        op.tick = None
        op.gid = len(self.ops)
        deps = {}

        def dep(p):
            if p is None:
                return
            k = self._k(p)
            q = deps.get(k)
            if q is None or q.gid < p.gid:
                deps[k] = p

        for b in reads:
            dep(b.last_w)
        for b in writes:
            dep(b.last_w)
            for r in b.readers.values():
                dep(r)
        op.deps = list(deps.values())
        for b in reads:
            b.readers[self._k(op) if not is_dma else ("d", sem_key)] = op
        for b in writes:
            b.last_w = op
            b.readers = {}
        if is_dma:
            self.dma_count[sem_key] = self.dma_count.get(sem_key, 0) + 16
            op.tick = self.dma_count[sem_key]
        self.ops.append(op)
        return op

    def op(self, eng, fn, reads=(), writes=(), cost=500.0):
        self.rec.append((eng, fn, list(reads), list(writes), False, None, float(cost)))
        return len(self.rec) - 1

    def dma(self, queue, fn, reads=(), writes=(), key=None, group=False, cost=6000.0):
        self.rec.append((queue, fn, list(reads), list(writes), True, key, float(cost)))
        return len(self.rec) - 1

    def schedule(self, window=600, lat=150.0, wmap=None, slack=0.0):
        wmap = wmap or {}
        rec = self.rec
        n = len(rec)
        last_w = {}
        rdrs = {}
        preds = [None] * n
        for i, (eng, fn, reads, writes, is_dma, key, cost) in enumerate(rec):
            p = set()
            for b in reads:
                w = last_w.get(id(b))
                if w is not None:
                    p.add(w)
            for b in writes:
                w = last_w.get(id(b))
                if w is not None:
                    p.add(w)
                p.update(rdrs.get(id(b), ()))
            p.discard(i)
            preds[i] = p
            for b in reads:
                rdrs.setdefault(id(b), []).append(i)
            for b in writes:
                last_w[id(b)] = i
                rdrs[id(b)] = []
        succs = [[] for _ in range(n)]
        indeg = [0] * n
        for i in range(n):
            indeg[i] = len(preds[i])
            for p in preds[i]:
                succs[p].append(i)
        import heapq
        blevel = [0.0] * n
        for i in range(n - 1, -1, -1):
            m_ = 0.0
            for sx in succs[i]:
                if blevel[sx] > m_:
                    m_ = blevel[sx]
            blevel[i] = m_ + rec[i][6]
        ready = [i for i in range(n) if indeg[i] == 0]
        heapq.heapify(ready)
        cand = []
        efree = {e: 0.0 for e in self.ENGS}
        cur_set = [None]
        n_switch = [0]

        def act_pen(i):
            need = self.aset.get(i)
            if need is None:
                return 0.0
            c = cur_set[0]
            ok = (need == "tanh" and c in ("E", "Si", "Ge")) or (need == "gelu" and c == "Ge") or (need == "exp" and c in ("E", "L")) or (need == "ln" and c == "L") or \
                 (need == "sqrt" and c == "S") or (need == "silu" and c == "Si")
            return 0.0 if ok else 2600.0
        finish = [0.0] * n
        done = [False] * n
        order = []
        base = 0
        while len(order) < n:
            while base < n and done[base]:
                base += 1
            while ready and ready[0] < base + window:
                cand.append(heapq.heappop(ready))
            best = None
            bt = None
            tl = []
            for i in cand:
                eng = rec[i][0]
                if i >= base + wmap.get(eng, window):
                    continue
                t = efree[eng]
                if eng == "act":
                    t += act_pen(i)
                for p in preds[i]:
                    f = finish[p] + lat
                    if f > t:
                        t = f
                tl.append((t, i))
                if bt is None or t < bt - 1e-9 or (abs(t - bt) <= 1e-9 and i < best):
                    best, bt = i, t
            if slack > 0.0:
                bb = None
                for (t, i) in tl:
                    if t <= bt + slack and (bb is None or blevel[i] > blevel[bb] or (blevel[i] == blevel[bb] and i < bb)):
                        bb = i
                for (t, i) in tl:
                    if i == bb:
                        best, bt = i, t
                        break
            i = best
            cand.remove(i)
            eng, fn, reads, writes, is_dma, key, cost = rec[i]
            if eng == "act" and i in self.aset:
                if act_pen(i) > 0.0:
                    n_switch[0] += 1
                    need = self.aset[i]
                    cur_set[0] = {"tanh": "E", "exp": "E", "ln": "L", "sqrt": "S", "silu": "Si", "gelu": "Ge"}[need]
            if is_dma:
                efree[eng] = bt + 100.0
                finish[i] = bt + cost
            else:
                efree[eng] = bt + cost
                finish[i] = bt + cost
            done[i] = True
            order.append(i)
            for sidx in succs[i]:
                indeg[sidx] -= 1
                if indeg[sidx] == 0:
                    heapq.heappush(ready, sidx)
        self.est_total = max(finish) if n else 0.0
        self.n_switch = n_switch[0]
        return order

    def replay(self, order):
        rec = self.rec
        seen = set()
        for tup in rec:
            for b in tup[2] + tup[3]:
                if id(b) not in seen:
                    seen.add(id(b))
                    b.last_w = None
                    b.readers = {}
        self.ops = []
        self.dma_count = {}
        m = {}
        for i in order:
            eng, fn, reads, writes, is_dma, key, cost = rec[i]
            m[i] = self._add(eng, fn, reads, writes, is_dma=is_dma, sem_key=key)
        return m

    def _skip(self, p, o):
        return (not p.is_dma) and (not o.is_dma) and p.eng == o.eng and p.eng == "pe"

    def emit(self, final_wait_ops=()):
        nc = self.nc
        ops = self.ops
        for o in ops:
            for p in o.deps:
                if p.is_dma or self._skip(p, o):
                    continue
                p.needs_inc = True
        ticks = {e: 0 for e in self.ENGS}
        for o in ops:
            if not o.is_dma and o.needs_inc:
                ticks[o.eng] += 1
                o.tick = ticks[o.eng]
        per_eng = {e: [o for o in ops if o.eng == e] for e in self.ENGS}
        with ExitStack() as st:
            esem = {e: st.enter_context(nc.semaphore("s_" + e)) for e in ("pe", "act", "dve", "pool")}
            dsem = {}
            for i, k in enumerate(self.dma_count):
                dsem[k] = st.enter_context(nc.semaphore("d%d" % i))
            block = st.enter_context(nc.Block())

            def run(ename, eng):
                waited = {}
                for o in per_eng[ename]:
                    for p in o.deps:
                        if p.is_dma:
                            s = dsem[p.sem_key]
                            v = self.dma_count[p.sem_key] if p.sem_key in self.group_keys else p.tick
                        else:
                            if self._skip(p, o):
                                continue
                            s = esem[p.eng]
                            v = p.tick
                        kk = self._k(p)
                        if waited.get(kk, 0) >= v:
                            continue
                        eng.wait_ge(s, v)
                        waited[kk] = v
                    inst = o.fn(eng)
                    if o.is_dma:
                        inst.then_inc(dsem[o.sem_key], 16)
                    elif o.needs_inc:
                        inst.then_inc(esem[o.eng], 1)
                if ename == "sp":
                    for p in final_wait_ops:
                        v = self.dma_count[p.sem_key] if p.sem_key in self.group_keys else p.tick
                        eng.wait_ge(dsem[p.sem_key], v)

            @block.tensor
            def _(e):
                run("pe", e)

            @block.scalar
            def _(e):
                run("act", e)

            @block.vector
            def _(e):
                run("dve", e)

            @block.gpsimd
            def _(e):
                run("pool", e)

            @block.sync
            def _(e):
                run("sp", e)


def build_program(debug=(), ntiles=NT_PRE + NT_MAIN, cut=99):
    import os
    SSDCUT = int(os.environ.get('SSDCUT', '99'))
    nc = bass.Bass("TRN2", target_bir_lowering=False)
    S = Sched(nc)

    def din(name, shape, dt=F32):
        return nc.dram_tensor(name, shape, dt, kind="ExternalInput").ap()

    def dscr(name, shape, dt=BF16):
        return nc.dram_tensor(name, shape, dt, kind="Internal").ap()

    xT_d = din("xT", [D_MODEL, (NT_PRE + NT_MAIN) * TT_])
    par_d = din("par", [128, NPAR])
    cf_d = din("cf32", [128, 384])
    cb_d = din("cbf", [128, 256], BF16)
    out_d = nc.dram_tensor("outT", [D_MODEL, NT_MAIN * TT_], F32, kind="ExternalOutput").ap()

    WSPEC = {"win": (20, 4096), "wdt": (1, 256), "wos": (4, 4096), "wol": (4, 2560), "wout": (2, 4096),
             "wfi": (11, 4096), "wfo": (8, 2816), "wri": (1, 2560)}
    w_d = {k: din(k, [g, 128, n]) for k, (g, n) in WSPEC.items()}
    w_s = {k: dscr(k + "_s", [g, 128, n]) for k, (g, n) in WSPEC.items()}

    dbg_out = {}

    with ExitStack() as st:
        def sb(name, shape, dt):
            return st.enter_context(nc.sbuf_tensor("sb_" + name, shape, dt))

        def ps(name, shape, dt):
            return st.enter_context(nc.psum_tensor("ps_" + name, shape, dt))

        par = sb("par", [128, NPAR], F32); b_par = Buf("par")
        cf = sb("cf", [128, 384], F32); b_cf = Buf("cf")
        cbf = sb("cbf", [128, 256], BF16); b_cbf = Buf("cbf")
        tri = cf[:, 0:128]
        Pm = cf[:, 128:256]
        onesf = cf[:, 256:384]
        ident = cbf[:, 0:128]
        onesb = cbf[:, 128:256]
        der = sb("der", [128, 256], F32); b_der = Buf("der")
        wdt = sb("wdt", [128, 8, 32], BF16); b_wdt = Buf("wdt")
        XH = [sb("XH%d" % i, [128, 8, TT_], F32) for i in range(2)]
        b_XH = [Buf("XH%d" % i) for i in range(2)]
        hnT = sb("hnT", [128, 8, TT_], BF16); b_hn = Buf("hnT")
        BIG = sb("BIG", [128, 24, TT_], BF16); b_BIG = [Buf("BIG%d" % i) for i in range(24)]
        ZS = sb("ZS", [128, 4, 2048], BF16); b_ZS = [Buf("ZS%d" % i) for i in range(4)]
        oT = ZS[:].rearrange("p a b -> p (a b)")[:, 0:10 * TT_].rearrange("p (a b) -> p a b", a=10)
        ynT = sb("ynT", [128, 16, TT_], BF16); b_yn = [Buf("ynT%d" % i) for i in range(4)]
        mT = sb("mT", [128, 8, TT_], BF16); b_mT = [Buf("mT%d" % i) for i in range(8)]
        NR = 4
        ring = [sb("ring%d" % i, [128, 4096], BF16) for i in range(NR)]
        b_ring = [Buf("ring%d" % i) for i in range(NR)]
        xs_tok = sb("xs_tok", [128, 2048], BF16); b_xs = Buf("xs_tok")
        xdt = sb("xdt", [128, 2048], BF16); b_xdt = Buf("xdt")
        xdd = sb("xdd", [128, 2048], BF16); b_xdd = Buf("xdd")
        B_tok = sb("B_tok", [128, 512], BF16); b_Bt = Buf("B_tok")
        cbTm = sb("cbTm", [128, 4, 128], BF16); b_cbT = Buf("cbTm")
        Rt = sb("Rt", [128, 8, 128], F32); b_R = [Buf("R0"), Buf("R1")]
        Lt = sb("Lt", [128, 8, 128], BF16); b_Lh = [Buf("L0"), Buf("L1")]
        Mt = sb("Mt", [128, 16, 128], BF16); b_M = [Buf("M%d" % i) for i in range(4)]
        Sst = sb("Sst", [128, 2048], F32); b_S = [Buf("S%d" % i) for i in range(4)]
        Sbf = sb("Sbf", [128, 2048], BF16); b_Sb = [Buf("Sb%d" % i) for i in range(4)]
        TMP = [sb("tmp%d" % i, [128, TT_], F32) for i in range(10)]
        b_T = [Buf("tmp%d" % i) for i in range(10)]
        ynb2 = [sb("ynb%d" % i, [128, 512], BF16) for i in range(2)]; b_ynb2 = [Buf("ynb0"), Buf("ynb1")]
        U = [sb("U%d" % i, [128, TT_ + 4], BF16) for i in range(3)]
        b_U = [Buf("U%d" % i) for i in range(3)]
        Hs = sb("Hs", [128, 24, 4], BF16); b_Hs = [Buf("Hs%d" % i) for i in range(24)]
        Hl = sb("Hl", [128, 10, 4], BF16); b_Hl = [Buf("Hl%d" % i) for i in range(10)]
        hlast = sb("hlast", [128, 10], F32); b_hl = [Buf("hl%d" % i) for i in range(10)]
        SM = sb("SM", [128, 1024], F32)
        dt_sb = SM[:, 0:128].rearrange("p (c h) -> p c h", c=4); b_dt = [Buf("dt%d" % i) for i in range(4)]
        dA_sb = SM[:, 128:256].rearrange("p (c h) -> p c h", c=4); b_dA = [Buf("dA%d" % i) for i in range(4)]
        cs_sb = SM[:, 256:320]; b_cs = Buf("cs")
        ecs_sb = SM[:, 320:352]; b_ecs = Buf("ecs")
        eend_sb = SM[:, 352:384]; b_eend = Buf("eend")
        dd_sb = SM[:, 384:416]; b_dd = Buf("dd")
        dtdec_sb = SM[:, 416:448]; b_dtdec = Buf("dtdec")
        sp1 = SM[:, 448:480]; b_sp1 = Buf("sp1")
        sp2 = SM[:, 480:512]; b_sp2 = Buf("sp2")
        sp3 = SM[:, 512:544]; b_sp3 = Buf("sp3")
        ssq2 = [SM[:, 544:545], SM[:, 548:549]]; b_ssq2 = [Buf("ssq0"), Buf("ssq1")]
        sdv2 = [SM[:, 545:546], SM[:, 549:550]]; b_sd2 = [Buf("sd0"), Buf("sd1")]
        rsg2 = [SM[:, 546:547], SM[:, 550:551]]; b_rsg2 = [Buf("rsg0"), Buf("rsg1")]

        G = [ps("G%d" % i, [128, 512], F32) for i in range(4)]
        b_G = [Buf("G%d" % i) for i in range(4)]
        Dp = ps("Dp", [128, 1024], F32); b_Dp = [Buf("Dp0"), Buf("Dp1")]
        TR = [ps("TR%d" % i, [128, 1024], BF16) for i in range(2)]
        b_TR = [Buf("TR%d" % i) for i in range(2)]
        gstate = {"g": 0, "tr": 0, "ring": 0}

        def nbank():
            i = gstate["g"] % 4
            gstate["g"] += 1
            return G[i], b_G[i]

        def ntr():
            i = gstate["tr"] % 2
            gstate["tr"] += 1
            return TR[i], b_TR[i]

        def nel(ap):
            n = 1
            for d in list(ap.shape)[1:]:
                n *= int(d)
            return n

        def PE(out, lhsT, rhs, start, stop, r, w):
            f = 4.0 if lhsT.dtype == F32 else 1.0
            S.op("pe", lambda e: e.matmul(out, lhsT=lhsT, rhs=rhs, start=start, stop=stop), r, w,
                 cost=30.0 + 0.55 * f * nel(rhs))

        def PET(out, in_, r, w):
            S.op("pe", lambda e: e.transpose(out, in_, ident), list(r) + [b_cbf], w, cost=260.0)

        def ACT(out, in_, func, r, w, bias=None, scale=None, accum_out=None):
            kw = {}
            if bias is not None:
                kw["bias"] = bias
            if scale is not None:
                kw["scale"] = scale
            if accum_out is not None:
                kw["accum_out"] = accum_out
            idx_ = S.op("act", lambda e: e.activation(out=out, in_=in_, func=func, **kw), r, w, cost=200.0 + 0.85 * nel(out))
            need_ = {AF.Tanh: "tanh", AF.Exp: "exp", AF.Ln: "ln", AF.Sqrt: "sqrt", AF.Silu: "silu", AF.Gelu_apprx_tanh: "gelu"}.get(func)
            if need_:
                S.aset[idx_] = need_

        def TTo(eng, out, in0, in1, op, r, w):
            S.op(eng, lambda e: e.tensor_tensor(out=out, in0=in0, in1=in1, op=op), r, w,
                 cost=(250.0 + 1.9 * nel(out)) if eng == "pool" else (150.0 + 1.05 * nel(out)))

        def TS(eng, out, in0, s1, s2, op0, op1, r, w):
            cst = (250.0 + 1.9 * nel(out)) if eng == "pool" else (150.0 + 1.05 * nel(out))
            if op1 is None:
                S.op(eng, lambda e: e.tensor_scalar(out=out, in0=in0, scalar1=s1, scalar2=None, op0=op0), r, w, cost=cst)
            else:
                S.op(eng, lambda e: e.tensor_scalar(out=out, in0=in0, scalar1=s1, scalar2=s2, op0=op0, op1=op1), r, w,
                     cost=cst)

        def STT(out, in0, scalar, in1, op0, op1, r, w):
            S.op("dve", lambda e: e.scalar_tensor_tensor(out=out, in0=in0, scalar=scalar, in1=in1, op0=op0, op1=op1), r, w,
                 cost=150.0 + 1.05 * nel(out))

        def CP(eng, out, in_, r, w):
            if eng == "act":
                ACT(out, in_, AF.Copy, r, w)
            else:
                S.op(eng, lambda e: e.tensor_copy(out=out, in_=in_), r, w,
                     cost=(250.0 + 1.9 * nel(out)) if eng == "pool" else (150.0 + 1.05 * nel(out)))

        def dump(name, ap, shape, dt, r):
            if name not in debug:
                return
            d = nc.dram_tensor("dbg_" + name, list(shape), dt, kind="ExternalOutput").ap()
            dbg_out[name] = S.dma("sp", lambda e: e.dma_start(out=d, in_=ap), reads=r, key=("dbg", name))

        cast_groups = {"c_in0": [("win", g) for g in range(0, 6)] + [("wdt", 0)],
                       "c_in1": [("win", g) for g in range(6, 12)],
                       "c_wos": [("wos", g) for g in range(4)],
                       "c_in2": [("win", g) for g in range(12, 15)] + [("wri", 0)],
                       "c_in3": [("win", g) for g in range(15, 20)],
                       "c_wol": [("wol", g) for g in range(4)],
                       "c_wout": [("wout", g) for g in range(2)],
                       "c_wfi": [("wfi", g) for g in range(11)],
                       "c_wfo": [("wfo", g) for g in range(8)]}
        b_scr = {}
        gran_buf = {}
        S.dma("sp", lambda e: e.dma_start(out=par[:], in_=par_d[:, :]), writes=[b_par], key="par")
        S.dma("sp", lambda e: e.dma_start(out=cf[:], in_=cf_d[:, :]), writes=[b_cf], key="cf")
        S.dma("sp", lambda e: e.dma_start(out=cbf[:], in_=cb_d[:, :]), writes=[b_cbf], key="cbf")
        pre_list = [("win", g) for g in range(0, 6)] + [("wdt", 0)] + [("win", g) for g in (12, 13, 14)] + [("wri", 0)]
        main_list = ([("win", g) for g in range(6, 12)] + [("wos", g) for g in range(4)] + [("win", g) for g in range(15, 20)]
                     + [("wol", g) for g in range(4)] + [("wout", g) for g in range(2)] + [("wfi", g) for g in range(11)]
                     + [("wfo", g) for g in range(8)])
        bigflat = BIG[:].rearrange("p a b -> p (a b)")
        cstate = {"n": 0}

        def conv_load(wn, g, xi):
            n = WSPEC[wn][1]
            xf = XH[xi][:].rearrange("p a b -> p (a b)")
            S.dma("sp", (lambda e, d=xf[:, 0:n], s_=w_d[wn][g]: e.dma_start(out=d, in_=s_)), writes=[b_XH[xi]],
                  key=("x", xi), cost=2000.0 + n * 512 / 200.0)

        def conv_cast_store(wn, g, xi, eng):
            n = WSPEC[wn][1]
            xf = XH[xi][:].rearrange("p a b -> p (a b)")
            j = cstate["n"] % 3
            cstate["n"] += 1
            dstb = bigflat[:, j * 4096:j * 4096 + n]
            bdst = b_BIG[8 * j:8 * j + 8]
            CP(eng, dstb, xf[:, 0:n], [b_XH[xi]], bdst)
            gran_buf[(wn, g)] = Buf("scr_%s_%d" % (wn, g))
            S.dma("sp", (lambda e, d=w_s[wn][g], s_=dstb: e.dma_start(out=d, in_=s_)), reads=bdst,
                  writes=[gran_buf[(wn, g)]], key=("sto", j), cost=2000.0 + n * 256 / 200.0)

        cast_engs = ["pool", "dve", "act"]
        conv_load(pre_list[0][0], pre_list[0][1], 0)
        for i in range(len(pre_list)):
            if i + 1 < len(pre_list):
                conv_load(pre_list[i + 1][0], pre_list[i + 1][1], (i + 1) % 2)
            conv_cast_store(pre_list[i][0], pre_list[i][1], i % 2, cast_engs[i % 3])

        def load_gran(wn, g):
            i = gstate["ring"] % NR
            gstate["ring"] += 1
            n = WSPEC[wn][1]
            t = ring[i]
            src = w_s[wn][g]
            S.dma("sp", lambda e: e.dma_start(out=t[:, 0:n], in_=src), reads=[gran_buf[(wn, g)]], writes=[b_ring[i]],
                  key=("ring", i), cost=2000.0 + n * 256 / 200.0)
            return t, b_ring[i]

        S.dma("sp", lambda e: e.dma_start(out=wdt[:].rearrange("p k c -> p (k c)"), in_=w_s["wdt"][0]),
              reads=[gran_buf[("wdt", 0)]], writes=[b_wdt], key="wdt")

        w1 = par[:, 0:8]; w2 = par[:, 8:16]; wf = par[:, 16:24]; bg = par[:, 24:40]
        cws = par[:, 40:136]; cbs = par[:, 136:160]; cwl = par[:, 160:200]; cbl = par[:, 200:210]
        br = par[:, 210:220]; bi = par[:, 220:230]; lam = par[:, 230:240]; nw = par[:, 240:256]
        dtb = par[:, 256:288]; alog = par[:, 288:320]; Drep = par[:, 320:352]; smask = par[:, 352:353]
        A_rep = der[:, 0:32]; ca = der[:, 32:42]; hca = der[:, 42:52]
        mhalf = der[:, 52:53]; hbr = der[:, 54:64]; hbi = der[:, 64:74]; hbg = der[:, 74:90]
        cws_h = der[:, 90:186]; cbs_h = der[:, 186:210]

        ACT(der[:, 0:32], alog, AF.Exp, [b_par], [b_der])
        TS("dve", der[:, 0:32], der[:, 0:32], -1.0, None, ALU.mult, None, [b_der], [b_der])
        ACT(sp1[:, 0:10], lam, AF.Abs, [b_par], [b_sp1])
        ACT(sp1[:, 0:10], sp1[:, 0:10], AF.Exp, [b_sp1], [b_sp1], scale=-1.0)
        ACT(sp1[:, 0:10], sp1[:, 0:10], AF.Ln, [b_sp1], [b_sp1], bias=1.0)
        TS("dve", sp2[:, 0:10], lam, -1.0, 0.0, ALU.mult, ALU.max, [b_par], [b_sp2])
        TTo("dve", sp1[:, 0:10], sp1[:, 0:10], sp2[:, 0:10], ALU.add, [b_sp1, b_sp2], [b_sp1])
        TS("dve", der[:, 32:42], sp1[:, 0:10], -8.0, None, ALU.mult, None, [b_sp1, b_der], [b_der])
        TS("dve", der[:, 42:52], sp1[:, 0:10], -4.0, None, ALU.mult, None, [b_sp1, b_der], [b_der])
        S.op("pool", lambda e: e.memset(der[:, 52:54], -0.5), [b_der], [b_der])
        TS("dve", der[:, 54:64], br, 0.5, None, ALU.mult, None, [b_par, b_der], [b_der])
        TS("dve", der[:, 64:74], bi, 0.5, None, ALU.mult, None, [b_par, b_der], [b_der])
        TS("dve", der[:, 74:90], bg, 0.5, None, ALU.mult, None, [b_par, b_der], [b_der])
        TS("dve", der[:, 90:186], cws, 0.5, None, ALU.mult, None, [b_par, b_der], [b_der])
        TS("dve", der[:, 186:210], cbs, 0.5, None, ALU.mult, None, [b_par, b_der], [b_der])
        S.op("pool", lambda e: e.memset(Sst[:], 0.0), [], b_S)
        S.op("pool", lambda e: e.memset(Sbf[:], 0.0), [], b_Sb)
        S.op("pool", lambda e: e.memset(Hs[:], 0.0), [], b_Hs)
        S.op("pool", lambda e: e.memset(Hl[:], 0.0), [], b_Hl)
        S.op("pool", lambda e: e.memset(hlast[:], 0.0), [], b_hl)

        def rmsnorm(xh, bxh, wcols, out_tile, bout, out_is_f32):
            for k in range(8):
                ACT(BIG[:, k, :], xh[:, k, :], AF.Square, [bxh], [b_BIG[k]])
            pb, bpb = nbank()
            for k in range(8):
                PE(pb[:, :], onesb, BIG[:, k, :], k == 0, k == 7, [b_BIG[k], b_cbf], [bpb])
            ACT(TMP[0][:], pb[:, :], AF.Ln, [bpb], [b_T[0]], bias=EPS, scale=1.0 / D_MODEL)
            ACT(TMP[1][:], TMP[0][:], AF.Exp, [b_T[0]], [b_T[1]], scale=-0.5)
            for k in range(8):
                STT(out_tile[:, k, :], xh[:, k, :], wcols[:, k:k + 1], TMP[1][:], ALU.mult, ALU.mult,
                    [bxh, b_T[1], b_par], [bout])

        def conv_block(pbank, bpbank, ui, H, bH, hb, cw_ap, cbias_ap, acc_i):
            u, bu = U[ui], b_U[ui]
            CP("act", u[:, 3:3 + TT_], pbank[:, :], [bpbank], [bu])
            CP("act", u[:, 0:3], H[:, hb, 0:3], [bH[hb]], [bu])
            acc, bacc = TMP[acc_i], b_T[acc_i]
            ACT(acc[:], pbank[:, :], AF.Identity, [bpbank, b_par, b_der], [bacc], bias=cbias_ap, scale=cw_ap[:, 3:4])
            for k in (2, 1, 0):
                STT(acc[:], u[:, k:k + TT_], cw_ap[:, k:k + 1], acc[:], ALU.mult, ALU.add, [bu, bacc, b_par, b_der], [bacc])
            CP("dve", H[:, hb, 0:3], u[:, TT_:TT_ + 3], [bu], [bH[hb]])
            return acc, bacc

        fin_ops = []
        for ti in range(ntiles):
            main = ti >= NT_PRE
            xh, bxh = XH[ti % 2], b_XH[ti % 2]
            t0 = ti * TT_
            src = xT_d.rearrange("(k p) t -> p k t", p=128)[:, :, t0:t0 + TT_]
            S.dma("sp", (lambda e, s=src, d=xh: e.dma_start(out=d[:], in_=s)), writes=[bxh], key=("x", ti % 2), cost=12500.0)
            if not main:
                zs_f32 = ZS[:].rearrange("p a b -> p (a b)").bitcast(F32)
                mt_flat = mT[:].rearrange("p a b -> p (a b)")
                per = (len(main_list) + NT_PRE - 1) // NT_PRE
                pending = list(main_list[ti * per:(ti + 1) * per])

                def conv_step():
                    if not pending:
                        return
                    wn_, g_ = pending.pop(0)
                    n_ = WSPEC[wn_][1]
                    S.dma("sp", (lambda e, d=zs_f32[:, 0:n_], s_=w_d[wn_][g_]: e.dma_start(out=d, in_=s_)), writes=b_ZS,
                          key="cvl", cost=2000.0 + n_ * 512 / 200.0)
                    CP("act", mt_flat[:, 0:n_], zs_f32[:, 0:n_], b_ZS, b_mT)
                    gran_buf[(wn_, g_)] = Buf("scr_%s_%d" % (wn_, g_))
                    S.dma("sp", (lambda e, d=w_s[wn_][g_], s_=mt_flat[:, 0:n_]: e.dma_start(out=d, in_=s_)), reads=b_mT,
                          writes=[gran_buf[(wn_, g_)]], key="cvs", cost=2000.0 + n_ * 256 / 200.0)
            else:
                pending = []

                def conv_step():
                    return
            rmsnorm(xh, bxh, w1, hnT, b_hn, False)
            if ti == NT_PRE:
                dump("hnT", hnT[:], [128, 8, TT_], BF16, [b_hn])

            if cut < 2:
                continue
            ngr = 6 if (main or ti == NT_PRE - 1) else 5
            blkc = 0
            for gi in range(ngr):
                wt, bw = load_gran("win", gi)
                w3 = wt[:].rearrange("p (k c) -> p k c", k=8)
                for bl in range(4):
                    cb = gi * 4 + bl
                    pb, bpb = nbank()
                    for k in range(8):
                        PE(pb[:, :], w3[:, k, bl * 128:(bl + 1) * 128], hnT[:, k, :], k == 0, k == 7, [bw, b_hn], [bpb])
                    acc, bacc = conv_block(pb, bpb, blkc % 3, Hs, b_Hs, cb, cws[:, cb * 4:cb * 4 + 4], cbs[:, cb:cb + 1],
                                           2 + blkc % 3)
                    ACT(BIG[:, cb, :], acc[:], AF.Silu, [bacc], [b_BIG[cb]])
                    blkc += 1
                conv_step()
            if ti == NT_PRE:
                dump("xbcT", BIG[:], [128, 24, TT_], BF16, b_BIG)

            if cut < 3:
                continue
            for c in range(4):
                pb, bpb = nbank()
                for k in range(8):
                    PE(pb[:, 0:32], hnT[:, k, c * 128:(c + 1) * 128], wdt[:, k, :], k == 0, k == 7, [b_hn, b_wdt], [bpb])
                TTo("dve", dt_sb[:, c, :], pb[:, 0:32], dtb, ALU.add, [bpb, b_par], [b_dt[c]])
            dtf = SM[:, 0:128]
            dAf = SM[:, 128:256]
            t4 = TMP[4][:, 0:128]
            t5 = TMP[5][:, 0:128]
            ACT(t4, dtf, AF.Abs, b_dt, [b_T[4]])
            ACT(t4, t4, AF.Exp, [b_T[4]], [b_T[4]], scale=-1.0)
            ACT(t4, t4, AF.Ln, [b_T[4]], [b_T[4]], bias=1.0)
            TS("dve", t5, dtf, 0.0, None, ALU.max, None, b_dt, [b_T[5]])
            TTo("dve", dtf, t4, t5, ALU.add, [b_T[4], b_T[5]], b_dt)
            TTo("pool", dAf.rearrange("p (c h) -> p c h", c=4), dtf.rearrange("p (c h) -> p c h", c=4),
                A_rep.unsqueeze(1).broadcast_to([128, 4, 32]), ALU.mult, b_dt + [b_der], b_dA)
            if main:
                for g in range(4):
                    wt, bw = load_gran("win", 6 + g)
                    w3 = wt[:].rearrange("p (k c) -> p k c", k=8)
                    for c in range(4):
                        pb, bpb = nbank()
                        for k in range(8):
                            PE(pb[:, :], hnT[:, k, c * 128:(c + 1) * 128], w3[:, k, :], k == 0, k == 7, [b_hn, bw], [bpb])
                        ACT(ZS[:, c, g * 512:(g + 1) * 512], pb[:, :], AF.Silu, [bpb], [b_ZS[c]])

            if cut < 4:
                continue
            for c in range(4):
                csl = slice(c * 128, (c + 1) * 128)
                for rnd in range(2):
                    tr, btr = ntr()
                    for j in range(8):
                        blk = rnd * 8 + j
                        PET(tr[:, j * 128:(j + 1) * 128], BIG[:, blk, csl], [b_BIG[blk]], [btr])
                    CP("act", xs_tok[:, rnd * 1024:(rnd + 1) * 1024], tr[:, :], [btr], [b_xs])
                if main:
                    TTo("dve", xdt[:].rearrange("p (h q) -> p h q", h=32), xs_tok[:].rearrange("p (h q) -> p h q", h=32),
                        dt_sb[:, c, :].unsqueeze(2).broadcast_to([128, 32, 64]), ALU.mult, [b_xs, b_dt[c]], [b_xdt])
                tr, btr = ntr()
                for j in range(4):
                    PET(tr[:, j * 128:(j + 1) * 128], BIG[:, 16 + j, csl], [b_BIG[16 + j]], [btr])
                CP("act", B_tok[:, :], tr[:, 0:512], [btr], [b_Bt])
                pb, bpb = nbank()
                PE(pb[:, 0:32], tri, dA_sb[:, c, :], True, True, [b_cf, b_dA[c]], [bpb])
                PE(pb[:, 32:64], onesf, dA_sb[:, c, :], True, True, [b_cf, b_dA[c]], [bpb])
                CP("dve", cs_sb, pb[:, 0:64], [bpb], [b_cs])
                ACT(eend_sb, cs_sb[:, 32:64], AF.Exp, [b_cs], [b_eend])
                TTo("dve", dd_sb, cs_sb[:, 32:64], cs_sb[:, 0:32], ALU.subtract, [b_cs], [b_dd])
                ACT(dd_sb, dd_sb, AF.Exp, [b_dd], [b_dd])
                TTo("dve", dtdec_sb, dd_sb, dt_sb[:, c, :], ALU.mult, [b_dd, b_dt[c]], [b_dtdec])
                TTo("pool", xdd[:].rearrange("p (h q) -> p h q", h=32), xs_tok[:].rearrange("p (h q) -> p h q", h=32),
                    dtdec_sb.unsqueeze(2).broadcast_to([128, 32, 64]), ALU.mult, [b_xs, b_dtdec], [b_xdd])
                if main and SSDCUT >= 2:
                    ACT(ecs_sb, cs_sb[:, 0:32], AF.Exp, [b_cs], [b_ecs])
                    pcb, bpcb = nbank()
                    for g in range(4):
                        PE(pcb[:, g * 128:(g + 1) * 128], BIG[:, 16 + g, csl], BIG[:, 20 + g, csl], True, True,
                           [b_BIG[16 + g], b_BIG[20 + g]], [bpcb])
                    for g in range(4):
                        TTo("dve", cbTm[:, g, :], pcb[:, g * 128:(g + 1) * 128], tri, ALU.mult, [bpcb, b_cf], [b_cbT])
                    v8 = lambda ap: ap.rearrange("p (h q) -> p h q", h=8)

                    def ssd_A(hg):
                        g, hh = hg // 2, hg % 2
                        h0 = 8 * g + 4 * hh
                        Rh = Rt[:, 4 * hh:4 * hh + 4, :]
                        TTo("pool", Rh, dA_sb[:, c, h0:h0 + 4].unsqueeze(2).broadcast_to([128, 4, 128]),
                            tri.unsqueeze(1).broadcast_to([128, 4, 128]), ALU.mult, [b_dA[c], b_cf], [b_R[hh]])
                        Dh = Dp[:, hh * 512:(hh + 1) * 512]
                        PE(Dh, Pm, Rh.rearrange("p a b -> p (a b)"), True, True, [b_cf, b_R[hh]], [b_Dp[hh]])
                        Lh = Lt[:, 4 * hh:4 * hh + 4, :]
                        ACT(Lh.rearrange("p a b -> p (a b)"), Dh, AF.Exp, [b_Dp[hh]], [b_Lh[hh]])
                        ms = hg % 4
                        TTo("dve", Mt[:, 4 * ms:4 * ms + 4, :], Lh, cbTm[:, g, :].unsqueeze(1).broadcast_to([128, 4, 128]),
                            ALU.mult, [b_Lh[hh], b_cbT], [b_M[ms]])

                    def ssd_B(g):
                        gsl = slice(g * 512, (g + 1) * 512)
                        pq = g % 2
                        TA, TB = 4 + 2 * pq, 5 + 2 * pq
                        pyo, bpyo = nbank()
                        PE(pyo[:, :], BIG[:, 20 + g, csl], Sbf[:, gsl], True, True, [b_BIG[20 + g], b_Sb[g]], [bpyo])
                        pyd, bpyd = nbank()
                        for r in range(8):
                            h = 8 * g + r
                            ms = (2 * g + r // 4) % 4
                            PE(pyd[:, r * 64:(r + 1) * 64], Mt[:, 4 * ms + r % 4, :], xdt[:, h * 64:(h + 1) * 64], True, True,
                               [b_M[ms], b_xdt], [bpyd])
                        TTo("dve", v8(TMP[TA][:]), v8(pyo[:, :]),
                            ecs_sb[:, 8 * g:8 * g + 8].unsqueeze(2).broadcast_to([128, 8, 64]), ALU.mult,
                            [bpyo, b_ecs], [b_T[TA]])
                        TTo("pool", v8(TMP[TB][:]), v8(xs_tok[:, gsl]),
                            Drep[:, 8 * g:8 * g + 8].unsqueeze(2).broadcast_to([128, 8, 64]), ALU.mult,
                            [b_xs, b_par], [b_T[TB]])
                        TTo("pool", TMP[TB][:], TMP[TB][:], TMP[TA][:], ALU.add, [b_T[TA], b_T[TB]], [b_T[TB]])
                        TTo("dve", TMP[TB][:], pyd[:, :], TMP[TB][:], ALU.add, [bpyd, b_T[TB]], [b_T[TB]])
                        TTo("dve", TMP[TA][:], TMP[TB][:], ZS[:, c, gsl], ALU.mult, [b_T[TB], b_ZS[c]], [b_T[TA]])
                        ACT(TMP[TB][:], TMP[TA][:], AF.Square, [b_T[TA]], [b_T[TB], b_ssq2[pq]], accum_out=ssq2[pq])
                        TS("dve", sdv2[pq], ssq2[pq], 1.0 / 512, EPS, ALU.mult, ALU.add, [b_ssq2[pq]], [b_sd2[pq]])
                        S.op("pool", lambda e: e.tensor_tensor(out=rsg2[pq], in0=sdv2[pq], in1=mhalf, op=ALU.pow),
                             [b_sd2[pq], b_der], [b_rsg2[pq]])
                        ACT(ynb2[pq][:, :], TMP[TA][:], AF.Copy, [b_T[TA], b_rsg2[pq]], [b_ynb2[pq]], scale=rsg2[pq])
                        tr, btr = ntr()
                        for j in range(4):
                            PET(tr[:, j * 128:(j + 1) * 128], ynb2[pq][:, j * 128:(j + 1) * 128], [b_ynb2[pq]], [btr])
                        for j in range(4):
                            ACT(ynT[:, 4 * g + j, csl], tr[:, j * 128:(j + 1) * 128], AF.Copy, [btr, b_par], [b_yn[g]],
                                scale=nw[:, 4 * g + j:4 * g + j + 1])

                    ssd_A(0); ssd_A(1); ssd_A(2); ssd_A(3)
                    ssd_B(0)
                    ssd_A(4); ssd_A(5)
                    ssd_B(1)
                    ssd_A(6); ssd_A(7)
                    ssd_B(2)
                    ssd_B(3)
                for g in range(4):
                    gsl = slice(g * 512, (g + 1) * 512)
                    pst, bpst = nbank()
                    PE(pst[:, :], B_tok[:, g * 128:(g + 1) * 128], xdd[:, gsl], True, True, [b_Bt, b_xdd], [bpst])
                    v8 = lambda ap: ap.rearrange("p (h q) -> p h q", h=8)
                    TTo("pool", v8(TMP[9][:]), v8(Sst[:, gsl]),
                        eend_sb[:, 8 * g:8 * g + 8].unsqueeze(2).broadcast_to([128, 8, 64]), ALU.mult,
                        [b_S[g], b_eend], [b_T[9]])
                    TTo("dve", Sst[:, gsl], pst[:, :], TMP[9][:], ALU.add, [bpst, b_T[9]], [b_S[g]])
                    if not (ti == NT_PRE - 1 and c == 3):
                        CP("act", Sbf[:, gsl], Sst[:, gsl], [b_S[g]], [b_Sb[g]])
            if ti == NT_PRE - 1:
                for g in range(4):
                    gsl = slice(g * 512, (g + 1) * 512)
                    TS("dve", Sst[:, gsl], Sst[:, gsl], smask, None, ALU.mult, None, [b_S[g], b_par], [b_S[g]])
                    CP("act", Sbf[:, gsl], Sst[:, gsl], [b_S[g]], [b_Sb[g]])
            if ti == NT_PRE:
                dump("ynT", ynT[:], [128, 16, TT_], BF16, b_yn)
                dump("S", Sst[:], [128, 2048], F32, b_S)

            if cut < 5:
                continue
            if main:
                wgs = {}
                wos = {}
                for j in range(8):
                    if j % 4 == 0:
                        wgs = load_gran("win", 10 + j // 4)
                    if j % 2 == 0:
                        wos = load_gran("wos", j // 2)
                    wo3 = wos[0][:].rearrange("p (k c) -> p k c", k=16)
                    wg3 = wgs[0][:].rearrange("p (k c) -> p k c", k=8)
                    pa, bpa = nbank()
                    for k in range(16):
                        PE(pa[:, :], wo3[:, k, (j % 2) * 128:(j % 2 + 1) * 128], ynT[:, k, :], k == 0, k == 15,
                           [wos[1], b_yn[k // 4]], [bpa])
                    pg, bpg = nbank()
                    for k in range(8):
                        PE(pg[:, :], wg3[:, k, (j % 4) * 128:(j % 4 + 1) * 128], hnT[:, k, :], k == 0, k == 7,
                           [wgs[1], b_hn], [bpg])
                    si = 4 + j % 2
                    ACT(TMP[si][:], pg[:, :], AF.Tanh, [bpg, b_der], [b_T[si]], bias=hbg[:, j:j + 1], scale=0.5)
                    STT(mT[:, j, :], TMP[si][:], 1.0, pa[:, :], ALU.add, ALU.mult, [bpa, b_T[si]], [b_mT[j]])
                if ti == NT_PRE:
                    dump("mssm", mT[:], [128, 8, TT_], BF16, b_mT)

            if cut < 6:
                continue
            wri_flat = ynT[:].rearrange("p a b -> p (a b)")[:, 0:2560]
            S.dma("sp", lambda e: e.dma_start(out=wri_flat, in_=w_s["wri"][0]), reads=[gran_buf[("wri", 0)]], writes=b_yn,
                  key="wri")
            wri3 = wri_flat.rearrange("p (a h j) -> p a h j", a=2, h=10)
            b_wri = b_yn
            wlx = wly = None
            lx_pre = None
            if not main:
                lx_pre = [load_gran("win", 12 + i) for i in range(3)]
            Ltf = Lt[:].rearrange("p a b -> p (a b)")

            def lru_s1(bl, sl):
                Tu, Ta, To, Ti = 2 + sl, 4 + 3 * sl, 5 + 3 * sl, 6 + 3 * sl
                wx3 = wlx[0][:].rearrange("p (k c) -> p k c", k=8)
                pb, bpb = nbank()
                for k in range(8):
                    PE(pb[:, :], wx3[:, k, (bl % 4) * 128:(bl % 4 + 1) * 128], hnT[:, k, :], k == 0, k == 7,
                       [wlx[1], b_hn], [bpb])
                u, bu = conv_block(pb, bpb, sl, Hl, b_Hl, bl, cwl[:, bl * 4:bl * 4 + 4], cbl[:, bl:bl + 1], Tu)
                ubf = Ltf[:, sl * TT_:(sl + 1) * TT_]
                CP("act", ubf, u[:], [bu], [b_Lh[sl]])
                pr, bpr = nbank()
                PE(pr[:, :], wri3[:, 0, bl, :], ubf, True, True, b_wri + [b_Lh[sl]], [bpr])
                pi_, bpi = nbank()
                PE(pi_[:, :], wri3[:, 1, bl, :], ubf, True, True, b_wri + [b_Lh[sl]], [bpi])
                ACT(TMP[Ta][:], pr[:, :], AF.Tanh, [bpr, b_der], [b_T[Ta]], bias=hbr[:, bl:bl + 1], scale=0.5)
                ACT(TMP[To][:], TMP[Ta][:], AF.Exp, [b_T[Ta], b_der], [b_T[To]], bias=ca[:, bl:bl + 1], scale=ca[:, bl:bl + 1])
                ACT(TMP[Ta][:], TMP[Ta][:], AF.Exp, [b_T[Ta], b_der], [b_T[Ta]], bias=hca[:, bl:bl + 1],
                    scale=hca[:, bl:bl + 1])
                ACT(TMP[To][:], TMP[To][:], AF.Relu, [b_T[To]], [b_T[To]], bias=1.0, scale=-1.0)
                ACT(TMP[Ti][:], pi_[:, :], AF.Tanh, [bpi, b_der], [b_T[Ti]], bias=hbi[:, bl:bl + 1], scale=0.5)
                STT(TMP[Ti][:], TMP[Ti][:], 1.0, u[:], ALU.add, ALU.mult, [b_T[Ti], bu], [b_T[Ti]])

            def lru_sq(sl):
                To = 5 + 3 * sl
                ACT(TMP[To][:], TMP[To][:], AF.Sqrt, [b_T[To]], [b_T[To]], bias=1e-30, scale=1.0)

            def lru_s2(bl, sl):
                Tu, Ta, To, Ti = 2 + sl, 4 + 3 * sl, 5 + 3 * sl, 6 + 3 * sl
                STT(TMP[Ti][:], TMP[Ti][:], 0.5, TMP[To][:], ALU.mult, ALU.mult, [b_T[Ti], b_T[To]], [b_T[Ti]])
                S.op("dve", (lambda e: e.tensor_tensor_scan(out=TMP[Tu][:], data0=TMP[Ta][:], data1=TMP[Ti][:],
                                                            initial=hlast[:, bl:bl + 1], op0=ALU.mult, op1=ALU.add)),
                     [b_T[Ta], b_T[Ti], b_hl[bl]], [b_T[Tu]], cost=1250.0)
                CP("dve", hlast[:, bl:bl + 1], TMP[Tu][:, TT_ - 1:TT_], [b_T[Tu]], [b_hl[bl]])
                if main:
                    wy3 = wly[0][:].rearrange("p (k c) -> p k c", k=8)
                    py, bpy = nbank()
                    for k in range(8):
                        PE(py[:, :], wy3[:, k, (bl % 4) * 128:(bl % 4 + 1) * 128], hnT[:, k, :], k == 0, k == 7,
                           [wly[1], b_hn], [bpy])
                    ACT(TMP[To][:], py[:, :], AF.Gelu_apprx_tanh, [bpy], [b_T[To]])
                    TTo("dve", oT[:, bl, :], TMP[Tu][:], TMP[To][:], ALU.mult, [b_T[Tu], b_T[To]], b_ZS)

            for pr_ in range(5):
                if (2 * pr_) % 4 == 0:
                    if main:
                        wlx = load_gran("win", 12 + (2 * pr_) // 4)
                        wly = load_gran("win", 15 + (2 * pr_) // 4)
                    else:
                        wlx = lx_pre[(2 * pr_) // 4]
                lru_s1(2 * pr_, 0)
                lru_s1(2 * pr_ + 1, 1)
                lru_sq(0)
                lru_sq(1)
                lru_s2(2 * pr_, 0)
                lru_s2(2 * pr_ + 1, 1)
                conv_step()
            while pending:
                conv_step()
            if ti == NT_PRE - 1:
                TS("dve", hlast[:], hlast[:], smask, None, ALU.mult, None, b_hl + [b_par], b_hl)
            if ti == NT_PRE:
                dump("oT", oT, [128, 10, TT_], BF16, b_ZS)

            if not main or cut < 7:
                continue
            wgl = wol = None
            for j in range(8):
                if j % 4 == 0:
                    wgl = load_gran("win", 18 + j // 4)
                if j % 2 == 0:
                    wol = load_gran("wol", j // 2)
                wo3 = wol[0][:, 0:2560].rearrange("p (k c) -> p k c", k=10)
                wg3 = wgl[0][:].rearrange("p (k c) -> p k c", k=8)
                pa, bpa = nbank()
                for k in range(10):
                    PE(pa[:, :], wo3[:, k, (j % 2) * 128:(j % 2 + 1) * 128], oT[:, k, :], k == 0, k == 9,
                       [wol[1]] + b_ZS, [bpa])
                pg, bpg = nbank()
                for k in range(8):
                    PE(pg[:, :], wg3[:, k, (j % 4) * 128:(j % 4 + 1) * 128], hnT[:, k, :], k == 0, k == 7,
                       [wgl[1], b_hn], [bpg])
                si = 4 + j % 2
                ACT(TMP[si][:], pg[:, :], AF.Tanh, [bpg, b_der], [b_T[si]], bias=hbg[:, 8 + j:9 + j], scale=0.5)
                STT(TMP[6 + j % 2][:], TMP[si][:], 1.0, pa[:, :], ALU.add, ALU.mult, [bpa, b_T[si]], [b_T[6 + j % 2]])
                TTo("dve", mT[:, j, :], TMP[6 + j % 2][:], mT[:, j, :], ALU.add, [b_T[6 + j % 2], b_mT[j]], [b_mT[j]])
            if ti == NT_PRE:
                dump("mT", mT[:], [128, 8, TT_], BF16, b_mT)
            wo = None
            for j in range(8):
                if j % 4 == 0:
                    wo = load_gran("wout", j // 4)
                w3 = wo[0][:].rearrange("p (k c) -> p k c", k=8)
                pa, bpa = nbank()
                for k in range(8):
                    PE(pa[:, :], w3[:, k, (j % 4) * 128:(j % 4 + 1) * 128], mT[:, k, :], k == 0, k == 7,
                       [wo[1], b_mT[k]], [bpa])
                STT(xh[:, j, :], pa[:, :], 0.5, xh[:, j, :], ALU.mult, ALU.add, [bpa, bxh], [bxh])
            if ti == NT_PRE:
                dump("h1", xh[:], [128, 8, TT_], F32, [bxh])
            if cut < 8:
                continue
            rmsnorm(xh, bxh, w2, hnT, b_hn, False)
            for gi in range(11):
                wt, bw = load_gran("wfi", gi)
                w3 = wt[:].rearrange("p (k c) -> p k c", k=8)
                for q in range(2):
                    jb = 2 * gi + q
                    pa, bpa = nbank()
                    for k in range(8):
                        PE(pa[:, :], w3[:, k, q * 128:(q + 1) * 128], hnT[:, k, :], k == 0, k == 7, [bw, b_hn], [bpa])
                    pu, bpu = nbank()
                    for k in range(8):
                        PE(pu[:, :], w3[:, k, 256 + q * 128:256 + (q + 1) * 128], hnT[:, k, :], k == 0, k == 7,
                           [bw, b_hn], [bpu])
                    si = 4 + jb % 2
                    ACT(TMP[si][:], pa[:, :], AF.Silu, [bpa], [b_T[si]])
                    TTo("dve", BIG[:, jb, :], pu[:, :], TMP[si][:], ALU.mult, [bpu, b_T[si]], [b_BIG[jb]])
            for j in range(8):
                wt, bw = load_gran("wfo", j)
                w3 = wt[:, 0:2816].rearrange("p (k c) -> p k c", k=22)
                pa, bpa = nbank()
                for k in range(22):
                    PE(pa[:, :], w3[:, k, :], BIG[:, k, :], k == 0, k == 21, [bw, b_BIG[k]], [bpa])
                TTo("dve", xh[:, j, :], pa[:, :], xh[:, j, :], ALU.add, [bpa, bxh], [bxh])
            rmsnorm(xh, bxh, wf, xh, bxh, True)
            o0 = (ti - NT_PRE) * TT_
            dst = out_d.rearrange("(k p) t -> p k t", p=128)[:, :, o0:o0 + TT_]
            fo = S.dma("sp", (lambda e, s=xh, d=dst: e.dma_start(out=d, in_=s[:])), reads=[bxh], key=("out", ti % 2),
                       cost=12500.0)
            fin_ops.append(fo)

        if os.environ.get("NO_RESCHED"):
            order = list(range(len(S.rec)))
        else:
            _w = int(os.environ.get('SWIN', '600'))
            _wm = {'sp': 64}
            _wm.update({e: int(os.environ['SWIN_' + e.upper()]) for e in Sched.ENGS if ('SWIN_' + e.upper()) in os.environ})
            order = S.schedule(window=_w, wmap=_wm, slack=float(os.environ.get('SSLACK', '75')), lat=float(os.environ.get('SLAT', '100')))
        m = S.replay(order)
        if os.environ.get('SCHED_VERBOSE'):
            _tot = {}
            for _r in S.rec:
                _tot[_r[0]] = _tot.get(_r[0], 0.0) + (_r[6] if not _r[4] else 100.0)
            print('sched: per-engine model busy us', {k: round(v / 1e3) for k, v in _tot.items()})
            print('sched: act switches(model)', getattr(S, 'n_switch', -1), 'ops', len(S.rec), 'est_total_us', getattr(S, 'est_total', 0) / 1e3, 'moved', sum(1 for k, v in enumerate(order) if k != v))
        S.emit(final_wait_ops=[m[i] for i in fin_ops + list(dbg_out.values())])
    return nc


def _gran(W, nk, cw):
    K, C = W.shape
    G = C // cw
    assert K == nk * 128 and G * cw == C
    return np.ascontiguousarray(W.reshape(nk, 128, G, cw).transpose(2, 1, 0, 3).reshape(G, 128, nk * cw))


def _pp(v, nb):
    return np.ascontiguousarray(np.asarray(v, np.float32).reshape(nb, 128).T)


def prepare_inputs(inp):
    import ml_dtypes
    f = lambda a: np.asarray(a, np.float32)
    w_in = f(inp["w_in"])[0]
    gates_w, z_w, xbc_w = w_in[:, 0:2048], w_in[:, 2048:4096], w_in[:, 4096:7168]
    dt_w, lx_w, ly_w = w_in[:, 7168:7200], w_in[:, 7200:8480], w_in[:, 8480:9760]
    pad = np.zeros((1024, 256), np.float32)
    win_perm = np.concatenate([xbc_w, z_w, gates_w[:, :1024], lx_w, pad, ly_w, pad, gates_w[:, 1024:]], axis=1)
    assert win_perm.shape[1] == 10240
    W = {}
    W["win"] = _gran(win_perm, 8, 512)
    W["wdt"] = _gran(dt_w, 8, 32)
    W["wos"] = _gran(f(inp["w_out_ssm"])[0], 16, 256)
    W["wol"] = _gran(f(inp["w_out_lru"])[0], 10, 256)
    W["wout"] = _gran(f(inp["w_out"])[0], 8, 512)
    wfi = f(inp["w_ffn_in"])[0]
    gate_w, up_w = wfi[:, :FFN_H], wfi[:, FFN_H:]
    cols = []
    for gi in range(11):
        cols.append(gate_w[:, gi * 256:(gi + 1) * 256])
        cols.append(up_w[:, gi * 256:(gi + 1) * 256])
    W["wfi"] = _gran(np.concatenate(cols, axis=1), 8, 512)
    W["wfo"] = _gran(f(inp["w_ffn_out"])[0], 22, 128)
    wr = f(inp["lru_w_r"])[0].transpose(1, 0, 2).reshape(128, 1280)
    wi = f(inp["lru_w_i"])[0].transpose(1, 0, 2).reshape(128, 1280)
    W["wri"] = np.ascontiguousarray(np.concatenate([wr, wi], axis=1)[None])

    par = np.zeros((128, NPAR), np.float32)
    par[:, 0:8] = _pp(inp["norm1_w"][0], 8)
    par[:, 8:16] = _pp(inp["norm2_w"][0], 8)
    par[:, 16:24] = _pp(inp["norm_f_w"], 8)
    par[:, 24:40] = _pp(inp["b_branch_gate"][0], 16)
    cw = f(inp["ssm_conv_w"])[0]
    par[:, 40:136] = cw.reshape(4, 24, 128).transpose(2, 1, 0).reshape(128, 96)
    par[:, 136:160] = _pp(inp["ssm_conv_b"][0], 24)
    cl = f(inp["lru_conv_w"])[0]
    par[:, 160:200] = cl.reshape(4, 10, 128).transpose(2, 1, 0).reshape(128, 40)
    par[:, 200:210] = _pp(inp["lru_conv_b"][0], 10)
    par[:, 210:220] = _pp(inp["lru_b_r"][0], 10)
    par[:, 220:230] = _pp(inp["lru_b_i"][0], 10)
    par[:, 230:240] = _pp(inp["lru_lambda"][0], 10)
    par[:, 240:256] = _pp(inp["ssm_norm_w"][0], 16)
    par[:, 256:288] = f(inp["ssm_dt_bias"])[0][None, :]
    par[:, 288:320] = f(inp["ssm_a_log"])[0][None, :]
    par[:, 320:352] = f(inp["ssm_d"])[0][None, :]

    kk = np.arange(128)
    tri = (kk[:, None] <= kk[None, :]).astype(np.float32)
    Pm = (kk[:, None] > kk[None, :]).astype(np.float32)
    cf = np.concatenate([tri, Pm, np.ones((128, 128), np.float32)], axis=1)
    cb = np.concatenate([np.eye(128, dtype=np.float32), np.ones((128, 128), np.float32)], axis=1).astype(ml_dtypes.bfloat16)

    x = f(inp["x"])
    in_maps = []
    for c in range(8):
        b, half = c // 2, c % 2
        xT = np.zeros((D_MODEL, 4096), np.float32)
        if half == 1:
            xT[:, 0:2048] = x[b, 0:2048].T
        xT[:, 2048:4096] = x[b, half * 2048:(half + 1) * 2048].T
        p = par.copy()
        p[:, 352] = float(half)
        m = {"xT": xT, "par": p, "cf32": cf, "cbf": cb}
        m.update(W)
        in_maps.append(m)
    return in_maps


_NC_CACHE = {}


def kernel(**inputs):
    in_maps = prepare_inputs(inputs)
    if "nc" not in _NC_CACHE:
        _NC_CACHE["nc"] = build_program()
    nc = _NC_CACHE["nc"]
    res = run_bass_kernel_spmd(nc, in_maps, core_ids=list(range(8)))
    out = np.zeros((BATCH, SEQ, D_MODEL), np.float32)
    for c in range(8):
        b, half = c // 2, c % 2
        out[b, half * 2048:(half + 1) * 2048, :] = np.asarray(res.results[c]["outT"], np.float32).T
    return out
```
